# Optimizing a Trainium2 kernel written in Bass

```python
import jax, jax.numpy as jnp
from jax import lax
import numpy as np

D_MODEL = 1024
BATCH = 8
SEQ = 4096
DEPTH = 2

D_FF = 2816
D_MIX = D_MODEL
HEAD_DIM = 64
N_HEADS = 8
N_KV_HEADS = 2
GROUP = N_HEADS // N_KV_HEADS
ATTN_W = N_HEADS * HEAD_DIM
KV_W = N_KV_HEADS * HEAD_DIM
CONV_CH = D_MIX - ATTN_W
CONV_WIDTH = 31
WINDOW = 128
BLOCK = 128
ROT_DIM = HEAD_DIM // 4
ROPE_THETA = 500000.0
PLE_DIM = 256
EPS = 1e-6
IN_W = ATTN_W + 2 * KV_W + 2 * CONV_CH
SPLITS = (ATTN_W, ATTN_W + KV_W, ATTN_W + 2 * KV_W, ATTN_W + 2 * KV_W + CONV_CH)

kernel_name = "hymba_swa_sink_conformer_conv_macaron"


def _rms_norm(x, g):
    xf = x.astype(jnp.float32)
    y = xf * lax.rsqrt(jnp.mean(xf * xf, axis=-1, keepdims=True) + EPS)
    return (y * g.astype(jnp.float32)).astype(x.dtype)


def _layer_norm(x, g, b):
    xf = x.astype(jnp.float32)
    mu = jnp.mean(xf, axis=-1, keepdims=True)
    xc = xf - mu
    y = xc * lax.rsqrt(jnp.mean(xc * xc, axis=-1, keepdims=True) + EPS)
    return (y * g.astype(jnp.float32) + b.astype(jnp.float32)).astype(x.dtype)


def _swiglu_ffn(x, w_gate_up, w_down):
    g, u = jnp.split(x @ w_gate_up, 2, axis=-1)
    return (jax.nn.silu(g) * u) @ w_down


def _partial_rope(x, positions):
    half = ROT_DIM // 2
    inv_freq = ROPE_THETA ** (-jnp.arange(0, ROT_DIM, 2, dtype=jnp.float32) / ROT_DIM)
    ang = positions.astype(jnp.float32)[..., None] * inv_freq
    cos = jnp.cos(ang)[:, :, None, :]
    sin = jnp.sin(ang)[:, :, None, :]
    xr = x[..., :ROT_DIM].astype(jnp.float32)
    x1, x2 = xr[..., :half], xr[..., half:]
    rot = jnp.concatenate([x1 * cos - x2 * sin, x2 * cos + x1 * sin], axis=-1).astype(x.dtype)
    return jnp.concatenate([rot, x[..., ROT_DIM:]], axis=-1)


def _sliding_window_attention(q, k, v, sinks):
    B, S, _, hd = q.shape
    nb = S // BLOCK
    qb = q.reshape(B, nb, BLOCK, N_KV_HEADS, GROUP, hd)
    kb = k.reshape(B, nb, BLOCK, N_KV_HEADS, hd)
    vb = v.reshape(B, nb, BLOCK, N_KV_HEADS, hd)
    pad = ((0, 0), (1, 0), (0, 0), (0, 0), (0, 0))
    kk = jnp.concatenate([jnp.pad(kb, pad)[:, :-1], kb], axis=2)
    vv = jnp.concatenate([jnp.pad(vb, pad)[:, :-1], vb], axis=2)
    s = jnp.einsum('bnqkgd,bnskd->bnkgqs', qb, kk).astype(jnp.float32) * (hd ** -0.5)
    qi = jnp.arange(BLOCK)[:, None] + BLOCK
    kj = jnp.arange(2 * BLOCK)[None, :]
    rel = qi - kj
    band = (rel >= 0) & (rel < WINDOW)
    valid = (jnp.arange(nb)[:, None, None] > 0) | (kj[None] >= BLOCK)
    mask = band[None] & valid
    s = jnp.where(mask[None, :, None, None], s, -1e30)
    sink = sinks.astype(jnp.float32).reshape(N_KV_HEADS, GROUP)[None, None, :, :, None, None]
    m = jnp.maximum(jnp.max(s, axis=-1, keepdims=True), sink)
    e = jnp.exp(s - m)
    probs = (e / (jnp.sum(e, axis=-1, keepdims=True) + jnp.exp(sink - m))).astype(v.dtype)
    o = jnp.einsum('bnkgqs,bnskd->bnqkgd', probs, vv)
    return o.reshape(B, S, N_HEADS * hd)


def _conformer_conv(a, g, w_dw, b_dw, ln_g, ln_b):
    u = a * jax.nn.sigmoid(g)
    y = lax.conv_general_dilated(
        u, w_dw[:, None, :].astype(u.dtype), window_strides=(1,),
        padding=[(CONV_WIDTH - 1, 0)],
        dimension_numbers=('NWC', 'WIO', 'NWC'),
        feature_group_count=CONV_CH) + b_dw
    return jax.nn.silu(_layer_norm(y, ln_g, ln_b))


def setup_inputs(seed: int = 0) -> dict:
    key = jax.random.key(seed)
    ks = jax.random.split(key, 24)
    f32 = jnp.float32

    def w(k, shape, fan_in, scale=1.0):
        return jax.random.normal(k, shape, f32) * (scale * fan_in ** -0.5)

    def gain(k, shape):
        return 1.0 + 0.02 * jax.random.normal(k, shape, f32)

    L = DEPTH
    x = jax.random.normal(ks[0], (BATCH, SEQ, D_MODEL), f32)
    p = jax.random.normal(ks[1], (DEPTH, BATCH, SEQ, PLE_DIM), f32)
    offsets = jax.random.randint(ks[2], (BATCH, 1), 0, 1024, dtype=jnp.int32)
    positions = offsets + jnp.arange(SEQ, dtype=jnp.int32)[None, :]
    return {
        "x": x,
        "p": p,
        "positions": positions,
        "ffn1_norm": gain(ks[3], (L, D_MODEL)),
        "ffn1_w_gate_up": w(ks[4], (L, D_MODEL, 2 * D_FF), D_MODEL),
        "ffn1_w_down": w(ks[5], (L, D_FF, D_MODEL), D_FF),
        "mix_norm": gain(ks[6], (L, D_MODEL)),
        "w_in": w(ks[7], (L, D_MODEL, IN_W), D_MODEL),
        "sinks": 0.5 * jax.random.normal(ks[8], (L, N_HEADS), f32),
        "conv_w": w(ks[9], (L, CONV_WIDTH, CONV_CH), CONV_WIDTH),
        "conv_b": 0.02 * jax.random.normal(ks[10], (L, CONV_CH), f32),
        "conv_ln_g": gain(ks[11], (L, CONV_CH)),
        "conv_ln_b": 0.02 * jax.random.normal(ks[12], (L, CONV_CH), f32),
        "attn_out_norm": gain(ks[13], (L, ATTN_W)),
        "conv_out_norm": gain(ks[14], (L, CONV_CH)),
        "w_out": w(ks[15], (L, D_MIX, D_MODEL), D_MIX),
        "ffn2_norm": gain(ks[16], (L, D_MODEL)),
        "ffn2_w_gate_up": w(ks[17], (L, D_MODEL, 2 * D_FF), D_MODEL),
        "ffn2_w_down": w(ks[18], (L, D_FF, D_MODEL), D_FF),
        "w_ple": w(ks[19], (L, PLE_DIM, D_MODEL), PLE_DIM),
        "ple_norm": gain(ks[20], (L, D_MODEL)),
        "w_ple_gate": w(ks[21], (L, D_MODEL, D_MODEL), D_MODEL),
        "final_norm": gain(ks[22], (D_MODEL,)),
    }


def reference(x, p, positions, ffn1_norm, ffn1_w_gate_up, ffn1_w_down, mix_norm, w_in,
              sinks, conv_w, conv_b, conv_ln_g, conv_ln_b, attn_out_norm, conv_out_norm,
              w_out, ffn2_norm, ffn2_w_gate_up, ffn2_w_down, w_ple, ple_norm, w_ple_gate,
              final_norm):
    B, S, _ = x.shape
    h = x
    for i in range(DEPTH):
        h = h + 0.5 * _swiglu_ffn(_rms_norm(h, ffn1_norm[i]), ffn1_w_gate_up[i], ffn1_w_down[i])
        u = _rms_norm(h, mix_norm[i])
        q, k, v, ca, cg = jnp.split(u @ w_in[i], SPLITS, axis=-1)
        q = _partial_rope(q.reshape(B, S, N_HEADS, HEAD_DIM), positions)
        k = _partial_rope(k.reshape(B, S, N_KV_HEADS, HEAD_DIM), positions)
        v = v.reshape(B, S, N_KV_HEADS, HEAD_DIM)
        attn = _sliding_window_attention(q, k, v, sinks[i])
        conv = _conformer_conv(ca, cg, conv_w[i], conv_b[i], conv_ln_g[i], conv_ln_b[i])
        mixed = jnp.concatenate([_rms_norm(attn, attn_out_norm[i]),
                                 _rms_norm(conv, conv_out_norm[i])], axis=-1)
        h = h + mixed @ w_out[i]
        h = h + 0.5 * _swiglu_ffn(_rms_norm(h, ffn2_norm[i]), ffn2_w_gate_up[i], ffn2_w_down[i])
        e = _rms_norm(p[i] @ w_ple[i], ple_norm[i])
        h = h + jax.nn.sigmoid(h @ w_ple_gate[i]) * e
    return _rms_norm(h, final_norm)
```

```python
import numpy as np
import concourse.bass as bass
import concourse.mybir as mybir
from concourse.bass_utils import run_bass_kernel_spmd

F32 = mybir.dt.float32
BF16 = mybir.dt.bfloat16
I32 = mybir.dt.int32
ALU = mybir.AluOpType
AF = mybir.ActivationFunctionType
ENGS = ('pe', 'act', 'dve', 'pool', 'sp')

D = 1024
SEQ = 4096
DFF = 2816
NJ = DFF // 128
T = 1024
TT = 512
NTL = T // TT
NBLK = T // 128
EPS = 1e-6
NLAYER = 2
CW = 31
WIN_CHUNKS = 20

O_FFN1, O_MIX, O_FFN2, O_PLE, O_ATTN, O_CONVN, O_LNG, O_LNB, O_CB, O_CW, O_SINK = 0, 8, 16, 24, 32, 40, 44, 48, 52, 56, 180
LW = 188
O_FINAL = NLAYER * LW
O_FREQ = O_FINAL + 8
O_PHC = O_FREQ + 1
O_PHS = O_FREQ + 2
NCON = O_FREQ + 4


class Sched:
    def __init__(s, nc, same_engine_sync=True):
        s.nc = nc
        s.streams = {e: [] for e in ENGS}
        s.esem = {e: nc.alloc_semaphore(name=f"sem_{e}") for e in ENGS if e != 'sp'}
        s.ecnt = {e: 0 for e in ENGS}
        s.obs = {e: {} for e in ENGS}
        s.lastw = {}
        s.readers = {}
        s.dcnt = {}
        s.same = same_engine_sync
        s.nsem = 0
        s.phase_keys = {}

    def new_dma_sem(s, name=None):
        s.nsem += 1
        h = s.nc.alloc_semaphore(name=name or f"dsem{s.nsem}")
        s.dcnt[h] = 0
        return h

    def op(s, eng, fn, reads=(), writes=(), inc=True, dma_sem=None):
        waits = {}
        own = s.esem.get(eng)

        def need(tok):
            if tok is None:
                return
            sem, val = tok
            if sem is own and (eng == 'pe' or not s.same):
                return
            if s.obs[eng].get(sem, 0) >= val:
                return
            if waits.get(sem, (None, 0))[1] < val:
                waits[sem] = (sem, val)
        for k in reads:
            need(s.lastw.get(k))
        for k in writes:
            need(s.lastw.get(k))
            for t in s.readers.get(k, ()):
                need(t)
        for sem, val in waits.values():
            s.obs[eng][sem] = val
        if dma_sem is not None:
            s.dcnt[dma_sem] += 16
            tok = (dma_sem, s.dcnt[dma_sem])
            incspec = (dma_sem, 16)
        elif inc:
            s.ecnt[eng] += 1
            tok = (own, s.ecnt[eng])
            incspec = (own, 1)
        else:
            tok = (own, s.ecnt[eng] + 1)
            incspec = None
        s.streams[eng].append((list(waits.values()), fn, incspec))
        for k in reads:
            s.readers.setdefault(k, []).append(tok)
        for k in writes:
            s.lastw[k] = tok
            s.readers[k] = []
        return tok

    def phase(s, buf, new_keys):
        old = s.phase_keys.get(buf, [])
        toks = []
        for k in old:
            if s.lastw.get(k) is not None:
                toks.append(s.lastw[k])
            toks.extend(s.readers.get(k, ()))
        best = {}
        for sem, val in toks:
            if best.get(sem, (None, 0))[1] < val:
                best[sem] = (sem, val)
        toks = list(best.values())
        for k in new_keys:
            s.readers.setdefault(k, []).extend(toks)
        s.phase_keys[buf] = list(new_keys)

    def final_wait(s, eng, toks):
        waits = {}
        for sem, val in toks:
            if waits.get(sem, (None, 0))[1] < val:
                waits[sem] = (sem, val)
        s.streams[eng].append((list(waits.values()), None, None))

    def emit(s):
        nc = s.nc
        with nc.Block() as block:
            def run(engname):
                def body(e):
                    for waits, fn, incspec in s.streams[engname]:
                        for sem, val in waits:
                            e.wait_ge(sem, val)
                        if fn is None:
                            continue
                        ins = fn(e)
                        if incspec is not None:
                            ins.then_inc(incspec[0], incspec[1])
                return body
            block.tensor(run('pe'))
            block.scalar(run('act'))
            block.vector(run('dve'))
            block.gpsimd(run('pool'))
            block.sync(run('sp'))


def build_program(nst=SEQ // T, nlayer=NLAYER, seq=SEQ):
    nc = bass.Bass("TRN2", target_bir_lowering=False)

    def din(name, shape, dt=F32):
        return nc.dram_tensor(name, shape, dt, kind="ExternalInput").ap()
    xT = din("xT", [D, seq])
    pT = din("pT", [NLAYER, 256, seq])
    pos = din("pos", [1, seq], I32)
    wgu = [din("wgu1", [NLAYER, D, 2 * DFF]), din("wgu2", [NLAYER, D, 2 * DFF])]
    wdn = [din("wd1", [NLAYER, DFF, D]), din("wd2", [NLAYER, DFF, D])]
    win = din("win", [NLAYER, D, WIN_CHUNKS * 128])
    wout = din("wout", [NLAYER, D, D])
    wple = din("wple", [NLAYER, 256, D])
    wgate = din("wgate", [NLAYER, D, D])
    con_d = din("con", [128, NCON])
    mask_d = din("mask", [128, 2, 128])
    ident_d = din("ident", [128, 128])
    outT = nc.dram_tensor("outT", [D, seq], F32, kind="ExternalOutput").ap()

    S = Sched(nc)
    sb = nc.alloc_sbuf_tensor
    hT = sb("hT", [128, 8, T], F32)
    xn = sb("xn", [128, 8, T], BF16)
    R1 = sb("R1", [128, 12288], F32)
    hid = R1[:, 0:11264].bitcast(BF16).rearrange("p (j t) -> p j t", j=NJ)
    UW = 1056
    ubf = R1[:, 0:2112].bitcast(BF16).rearrange("p (c t) -> p c t", c=4)
    yconv = R1[:, 2112:2112 + 4 * T].rearrange("p (c t) -> p c t", c=4)
    qT = R1[:, 6208:6208 + 2 * T].bitcast(BF16).rearrange("p (c t) -> p c t", c=4)
    Dg = [R1[:, 8256 + i * 1984:8256 + (i + 1) * 1984].bitcast(BF16).rearrange("p (j m) -> p j m", j=CW) for i in range(2)]
    eT = R1[:, 0:8 * T].rearrange("p (c t) -> p c t", c=8)
    MB = sb("MB", [128, 4 * T], BF16)
    mixedB = MB[:, :].rearrange("p (c t) -> p c t", c=4)
    wpl = MB[:, 0:2 * D].rearrange("p (k n) -> p k n", k=2)
    ppT = MB[:, 2 * D:2 * D + 2 * T].rearrange("p (k t) -> p k t", k=2)
    kT = [sb(f"kT{l}", [128, 2, 128 + T], BF16) for l in range(nlayer)]
    vaug = [sb(f"vaug{l}", [128, NBLK + 1, 2, 128], BF16) for l in range(nlayer)]
    uhalo = [sb(f"uhalo{l}", [128, 4, 32], BF16) for l in range(nlayer)]
    cosT = sb("cosT", [128, T], F32)
    sinT = sb("sinT", [128, T], F32)
    sqb = sb("sqb", [128, 8, TT], BF16)
    NRS = 2
    rs = [sb(f"rs{i}", [128, TT], F32) for i in range(NRS)]
    NTMP = 4
    tmp = [sb(f"tmp{i}", [128, TT], F32) for i in range(NTMP)]
    maskneg = sb("maskneg", [128, 2, TT], BF16)
    NET = 8
    etb = [sb(f"et{i}", [128, TT], BF16) for i in range(NET)]
    attn_blk = [sb(f"attn_blk{i}", [128, 4, 128], F32) for i in range(2)]
    sqA = [sb(f"sqA{i}", [128, 4, 128], BF16) for i in range(2)]
    gAfull = sb("gAfull", [128, 4, 128], F32)
    identb = sb("identb", [128, 128], BF16)
    con = sb("con_sb", [128, NCON], F32)
    ones = sb("ones", [128, 128], BF16)
    maskb = sb("mask_sb", [128, 2, 128], BF16)
    es = sb("es", [128, NLAYER, 8], F32)
    NA = 5
    Asl = [sb(f"Asl{i}", [128, 8, 256], BF16) for i in range(NA)]
    NB = 3
    Bsl = [sb(f"Bsl{i}", [128, NJ, 128], BF16) for i in range(NB)]
    NPS = 8
    ps = [nc.alloc_psum_tensor(f"ps{i}", [128, TT], F32) for i in range(NPS)]

    A_sem = [S.new_dma_sem() for _ in range(NA)]
    B_sem = [S.new_dma_sem() for _ in range(NB)]
    sem_wpl = S.new_dma_sem()
    sem_pp = S.new_dma_sem()
    sem_x = S.new_dma_sem()
    sem_c = S.new_dma_sem()
    sem_pos = [S.new_dma_sem() for _ in range(NTL)]
    sem_m = S.new_dma_sem()
    sem_i = S.new_dma_sem()
    sem_out = S.new_dma_sem()

    st8 = {'ps': 0, 'tmp': 0, 'rs': 0, 'A': 0, 'B': 0, 'et': 0}

    def rot(kind, n):
        i = st8[kind]
        st8[kind] = (i + 1) % n
        return i

    def bank():
        i = rot('ps', NPS)
        return ps[i], ('ps', i)

    def gtmp():
        i = rot('tmp', NTMP)
        return tmp[i], ('tmp', i)

    def grs():
        i = rot('rs', NRS)
        return rs[i], ('rs', i)

    def ts(t):
        return slice(t * TT, (t + 1) * TT)

    def mm_group(out_ap, out_key, terms, first=True, last=True):
        n = len(terms)
        for i, (l, r, rk) in enumerate(terms):
            st_ = first and i == 0
            sp_ = last and i == n - 1
            S.op('pe', lambda e, l=l, r=r, st_=st_, sp_=sp_: e.matmul(out_ap, lhsT=l, rhs=r, start=st_, stop=sp_),
                 reads=rk, writes=[out_key], inc=sp_)

    def loadA(src_ap, view=None):
        i = rot('A', NA)
        dst = Asl[i][:, :, :] if view is None else view(Asl[i])
        S.op('pool', lambda e: e.dma_start(out=dst, in_=src_ap), writes=[('A', i)], dma_sem=A_sem[i])
        return Asl[i], ('A', i)

    def loadB(src_ap):
        i = rot('B', NB)
        S.op('pool', lambda e: e.dma_start(out=Bsl[i][:, :, :], in_=src_ap), writes=[('B', i)], dma_sem=B_sem[i])
        return Bsl[i], ('B', i)

    def gcol(off, c=None):
        if c is None:
            return con[:, off:off + 1]
        return con[:, off + c:off + c + 1]

    S.op('sp', lambda e: e.dma_start(out=con[:, :], in_=con_d), writes=['con'], dma_sem=sem_c)
    S.op('pool', lambda e: e.dma_start(out=maskb[:, :, :], in_=mask_d), writes=['mask'], dma_sem=sem_m)
    S.op('dve', lambda e: e.memset(ones[:, :], 1.0), writes=['ones'])
    for kb in range(2):
        S.op('dve', lambda e, kb=kb: e.tensor_scalar(
            out=maskneg[:, kb, :].rearrange("p (h q) -> p h q", h=4), in0=maskb[:, kb, :].unsqueeze(1).broadcast_to([128, 4, 128]),
            scalar1=-1.0, scalar2=30000.0, op0=ALU.add, op1=ALU.mult), reads=['mask'], writes=['maskneg'])
    S.op('pool', lambda e: e.dma_start(out=identb[:, :], in_=ident_d), writes=['ident'], dma_sem=sem_i)
    for l in range(nlayer):
        S.op('act', lambda e, l=l: e.activation(out=es[:, l, :], in_=con[:, l * LW + O_SINK:l * LW + O_SINK + 8], func=AF.Exp),
             reads=['con'], writes=[('es', l)])
        S.op('dve', lambda e, l=l: e.memset(vaug[l][:, :, 0, 64:128], 1.0), writes=[('vones', l)])
        S.op('dve', lambda e, l=l: e.memset(vaug[l][:, :, 1, 0:64], 1.0), writes=[('vones', l)])
        S.op('dve', lambda e, l=l: e.memset(uhalo[l][:, :, :], 0.0), writes=[('uhalo', l)])
        S.op('dve', lambda e, l=l: e.memset(kT[l][:, :, :], 0.0), writes=[('k', l, 'halo'), ('k', l, 0), ('k', l, 1)])

    def rms_rstd(src3, src_keys, nchunk, kparts, denom, ncols):
        sq_view = sqb[0:kparts, 0:nchunk, 0:ncols]
        S.op('act', lambda e: e.activation(out=sq_view, in_=src3, func=AF.Square), reads=src_keys, writes=['sqb_lo', 'sqb_hi'])
        b, bk = bank()
        mm_group(b[:, 0:ncols], bk, [(ones[0:kparts, :], sqb[0:kparts, c, 0:ncols], ['sqb_lo', 'sqb_hi', 'ones']) for c in range(nchunk)])
        r, rk = grs()
        S.op('act', lambda e: e.activation(out=r[:, 0:ncols], in_=b[:, 0:ncols], func=AF.Ln, bias=EPS, scale=1.0 / denom),
             reads=[bk], writes=[rk])
        S.op('act', lambda e: e.activation(out=r[:, 0:ncols], in_=r[:, 0:ncols], func=AF.Exp, scale=-0.5), reads=[rk], writes=[rk])
        return r, rk

    def hkeys(t):
        return [('h', c, t) for c in range(8)]

    def xnkeys(t):
        return [('xn', c, t) for c in range(8)]

    early = {'skip0': False, 'hb0': False}
    XNKEYS = [('xn', c, t) for c in range(8) for t in range(NTL)]

    def norm_tile(goff, t):
        r, rk = rms_rstd(hT[:, :, ts(t)], hkeys(t), 8, 128, float(D), TT)
        for c in range(8):
            S.op('dve', lambda e, c=c, t=t, r=r: e.scalar_tensor_tensor(
                out=xn[:, c, ts(t)], in0=hT[:, c, ts(t)], scalar=gcol(goff, c), in1=r[:, :], op0=ALU.mult, op1=ALU.mult),
                reads=[('h', c, t), rk, 'con'], writes=[('xn', c, t)])

    def norm_to_xn(goff):
        if early['skip0']:
            early['skip0'] = False
            early['t1'] = lambda: norm_tile(goff, 1)
            return
        for t in range(NTL):
            norm_tile(goff, t)

    def flush_t1():
        f = early.get('t1')
        if f is not None:
            early['t1'] = None
            f()

    def early_norm(goff):
        S.phase('xnbuf', XNKEYS)
        norm_tile(goff, 0)
        early['skip0'] = True

    def early_hb():
        S.phase('xnbuf', XNKEYS)
        S.op('act', lambda e: e.activation(out=xn[:, :, ts(0)], in_=hT[:, :, ts(0)], func=AF.Copy), reads=hkeys(0), writes=xnkeys(0))
        early['hb0'] = True

    def wslice(w3, l, c0, c1):
        return w3[l].rearrange("(kc p) n -> p kc n", p=128)[:, :, c0:c1]

    def ffn(l, which, after_norm=None, tail_hook=None):
        goff = l * LW + (O_FFN1 if which == 0 else O_FFN2)
        S.phase('xnbuf', [('xn', c, t) for c in range(8) for t in range(NTL)])
        norm_to_xn(goff)
        S.phase('R1', [('hid', j, t) for j in range(NJ) for t in range(NTL)])
        W = wgu[which]
        uslots = {}

        def up_group(jp, jj, t):
            if jp not in uslots:
                uslots[jp] = (loadA(wslice(W, l, jp * 256, jp * 256 + 256)), loadA(wslice(W, l, DFF + jp * 256, DFF + jp * 256 + 256)))
            (ga, gk), (ua, uk) = uslots[jp]
            j = 2 * jp + jj
            gb, gbk = bank()
            ub, ubk = bank()
            mm_group(gb[:, :], gbk, [(ga[:, kc, jj * 128:(jj + 1) * 128], xn[:, kc, ts(t)], [gk, ('xn', kc, t)]) for kc in range(8)])
            mm_group(ub[:, :], ubk, [(ua[:, kc, jj * 128:(jj + 1) * 128], xn[:, kc, ts(t)], [uk, ('xn', kc, t)]) for kc in range(8)])
            sg, sgk = gtmp()
            S.op('act', lambda e: e.activation(out=sg[:, :], in_=gb[:, :], func=AF.Silu), reads=[gbk], writes=[sgk])
            S.op('dve', lambda e: e.tensor_tensor(out=hid[:, j, ts(t)], in0=ub[:, :], in1=sg[:, :], op=ALU.mult),
                 reads=[ubk, sgk], writes=[('hid', j, t)])

        up_order = [(jp, jj, 0) for jp in range(2) for jj in range(2)] + [(jp, jj, 1) for jp in range(2) for jj in range(2)]
        up_order += [(jp, jj, t) for jp in range(2, NJ // 2) for jj in range(2) for t in range(NTL)]
        for ui, (jp, jj, t) in enumerate(up_order):
            if ui == 4:
                flush_t1()
            up_group(jp, jj, t)
        if after_norm is not None:
            after_norm()
        Wd = wdn[which]
        order = [(m, t) for m in range(6) for t in range(NTL)] + [(6, 0), (7, 0), (6, 1), (7, 1)]
        bslots = {}
        for oi, (m, t) in enumerate(order):
            if m not in bslots:
                bslots[m] = loadB(Wd[l].rearrange("(jc p) n -> p jc n", p=128)[:, :, m * 128:(m + 1) * 128])
            ba, bk_ = bslots[m]
            ob, obk = bank()
            mm_group(ob[:, :], obk, [(ba[:, jc, :], hid[:, jc, ts(t)], [bk_, ('hid', jc, t)]) for jc in range(NJ)])
            S.op('dve', lambda e, ob=ob, m=m, t=t: e.scalar_tensor_tensor(
                out=hT[:, m, ts(t)], in0=ob[:, :], scalar=0.5, in1=hT[:, m, ts(t)], op0=ALU.mult, op1=ALU.add),
                reads=[obk, ('h', m, t)], writes=[('h', m, t)])
            if oi == 14 and tail_hook is not None:
                tail_hook()

    def rope_tables(st):
        t0 = st * T
        TWO_PI = float(2 * np.pi)
        for t in range(NTL):
            pi_t, pik = gtmp()
            pi_i = pi_t[:, :].bitcast(I32)
            S.op('sp', lambda e, pi_i=pi_i, t=t: e.dma_start(out=pi_i, in_=pos[:, t0 + t * TT:t0 + (t + 1) * TT].partition_broadcast(128)),
                 writes=[pik], dma_sem=sem_pos[t])
            pf, pfk = gtmp()
            S.op('dve', lambda e, pf=pf, pi_i=pi_i: e.tensor_copy(out=pf[:, :], in_=pi_i), reads=[pik], writes=[pfk])
            for tab, tk, phoff in ((cosT, 'cos', O_PHC), (sinT, 'sin', O_PHS)):
                dst = tab[:, ts(t)]
                key = (tk, t)
                a, ak = gtmp()
                ai = a[:, :].bitcast(I32)
                S.op('dve', lambda e, dst=dst, pf=pf, phoff=phoff: e.tensor_scalar(
                    out=dst, in0=pf[:, :], scalar1=gcol(O_FREQ), scalar2=gcol(phoff), op0=ALU.mult, op1=ALU.add),
                    reads=[pfk, 'con'], writes=[key])
                kf, kfk = gtmp()
                S.op('dve', lambda e, kf=kf, dst=dst: e.tensor_scalar(out=kf[:, :], in0=dst, scalar1=float(1.0 / TWO_PI), scalar2=None, op0=ALU.mult),
                     reads=[key], writes=[kfk])
                S.op('dve', lambda e, ai=ai, kf=kf: e.tensor_copy(out=ai, in_=kf[:, :]), reads=[kfk], writes=[ak])
                S.op('dve', lambda e, ai=ai, kf=kf: e.tensor_copy(out=kf[:, :], in_=ai), reads=[ak], writes=[kfk])
                S.op('dve', lambda e, dst=dst, kf=kf: e.scalar_tensor_tensor(out=dst, in0=kf[:, :], scalar=-TWO_PI, in1=dst, op0=ALU.mult, op1=ALU.add),
                     reads=[kfk, key], writes=[key])
                S.op('dve', lambda e, dst=dst, kf=kf: e.tensor_scalar(out=kf[:, :], in0=dst, scalar1=float(np.pi), scalar2=-TWO_PI, op0=ALU.is_gt, op1=ALU.mult),
                     reads=[key], writes=[kfk])
                S.op('dve', lambda e, dst=dst, kf=kf: e.tensor_tensor(out=dst, in0=dst, in1=kf[:, :], op=ALU.add), reads=[key, kfk], writes=[key])
                S.op('dve', lambda e, dst=dst, kf=kf: e.tensor_scalar(out=kf[:, :], in0=dst, scalar1=float(-np.pi), scalar2=TWO_PI, op0=ALU.is_lt, op1=ALU.mult),
                     reads=[key], writes=[kfk])
                S.op('dve', lambda e, dst=dst, kf=kf: e.tensor_tensor(out=dst, in0=dst, in1=kf[:, :], op=ALU.add), reads=[key, kfk], writes=[key])
                S.op('act', lambda e, dst=dst: e.activation(out=dst, in_=dst, func=AF.Sin), reads=[key], writes=[key])

    def mixer(l, st, tail_hook=None):
        cb = l * LW
        first_seq_block = (st == 0)
        S.phase('xnbuf', [('xn', c, t) for c in range(8) for t in range(NTL)])
        norm_to_xn(cb + O_MIX)
        ukeys = [('u', cc) for cc in range(4)]
        ykeys = [('y', cc, t) for cc in range(4) for t in range(NTL)]
        qkeys = [('q', c, t) for c in range(4) for t in range(NTL)]
        S.phase('R1', ukeys + ykeys + qkeys + [('dg', 0), ('dg', 1)])
        if st > 0:
            S.op('dve', lambda e: e.tensor_copy(out=kT[l][:, :, 0:128], in_=kT[l][:, :, T:T + 128]),
                 reads=[('k', l, NTL - 1)], writes=[('k', l, 'halo')])
            S.op('dve', lambda e: e.tensor_copy(out=vaug[l][:, 0, :, :], in_=vaug[l][:, NBLK, :, :]),
                 reads=[('v', l, NBLK), ('vones', l)], writes=[('v', l, 0)])
        for cc in range(4):
            S.op('dve', lambda e, cc=cc: e.tensor_copy(out=ubf[:, cc, 0:30], in_=uhalo[l][:, cc, 0:30]),
                 reads=[('uhalo', l)], writes=[('u', cc)])

        def wl(c0, n):
            return wslice(win, l, c0 * 128, (c0 + n) * 128)
        qslots = {}

        def qk_group(c, t):
            if c not in qslots:
                qslots[c] = loadA(wl(2 * c, 2))
            wa, wk = qslots[c]
            qb_, qbk = bank()
            sb_, sbk = bank()
            mm_group(qb_[:, :], qbk, [(wa[:, kc, 0:128], xn[:, kc, ts(t)], [wk, ('xn', kc, t)]) for kc in range(8)])
            mm_group(sb_[:, :], sbk, [(wa[:, kc, 128:256], xn[:, kc, ts(t)], [wk, ('xn', kc, t)]) for kc in range(8)])
            t1, t1k = gtmp()
            t2, t2k = gtmp()
            S.op('dve', lambda e, t1=t1, qb_=qb_, t=t: e.tensor_tensor(out=t1[:, :], in0=qb_[:, :], in1=cosT[:, ts(t)], op=ALU.mult),
                 reads=[qbk, ('cos', t)], writes=[t1k])
            S.op('dve', lambda e, t2=t2, sb_=sb_, t=t: e.tensor_tensor(out=t2[:, :], in0=sb_[:, :], in1=sinT[:, ts(t)], op=ALU.mult),
                 reads=[sbk, ('sin', t)], writes=[t2k])
            if c < 4:
                S.op('dve', lambda e, c=c, t=t, t1=t1, t2=t2: e.tensor_tensor(out=qT[:, c, ts(t)], in0=t1[:, :], in1=t2[:, :], op=ALU.add),
                     reads=[t1k, t2k], writes=[('q', c, t)])
            else:
                for g in range(2):
                    pr = slice(g * 64, (g + 1) * 64)
                    S.op('dve', lambda e, t=t, t1=t1, t2=t2, g=g, pr=pr: e.tensor_tensor(
                        out=kT[l][pr, g, 128 + t * TT:128 + (t + 1) * TT], in0=t1[pr, :], in1=t2[pr, :], op=ALU.add),
                        reads=[t1k, t2k], writes=[('k', l, t)])

        qk_order = [(0, 0), (1, 0)]
        for c, t in qk_order:
            qk_group(c, t)
        flush_t1()
        for c, t in [(0, 1), (1, 1)] + [(c, t) for c in range(2, 5) for t in range(NTL)]:
            qk_group(c, t)
        for cc in range(4):
            wa, wk = loadA(wl(10 + 2 * cc, 2))
            for t in range(NTL):
                ab, abk = bank()
                gb, gbk = bank()
                mm_group(ab[:, :], abk, [(wa[:, kc, 0:128], xn[:, kc, ts(t)], [wk, ('xn', kc, t)]) for kc in range(8)])
                mm_group(gb[:, :], gbk, [(wa[:, kc, 128:256], xn[:, kc, ts(t)], [wk, ('xn', kc, t)]) for kc in range(8)])
                sg, sgk = gtmp()
                S.op('act', lambda e, sg=sg, gb=gb: e.activation(out=sg[:, :], in_=gb[:, :], func=AF.Sigmoid), reads=[gbk], writes=[sgk])
                S.op('dve', lambda e, sg=sg, ab=ab, cc=cc, t=t: e.tensor_tensor(out=ubf[:, cc, 30 + t * TT:30 + (t + 1) * TT], in0=ab[:, :], in1=sg[:, :], op=ALU.mult),
                     reads=[abk, sgk], writes=[('u', cc)])
        wa, wk = loadA(wl(18, 2))
        for half in range(2):
            vb, vbk = bank()
            for bi in range(4):
                blk = half * 4 + bi
                t = blk // 4
                mm_group(vb[:, bi * 128:(bi + 1) * 128], vbk,
                         [(xn[:, kc, blk * 128:(blk + 1) * 128], wa[:, kc, 0:128], [wk, ('xn', kc, t)]) for kc in range(8)])
            for g in range(2):
                S.op('dve', lambda e, vb=vb, half=half, g=g: e.tensor_copy(
                    out=vaug[l][:, 1 + half * 4:1 + half * 4 + 4, g, g * 64:(g + 1) * 64],
                    in_=vb[:, :].rearrange("p (b g d) -> p b g d", b=4, g=2)[:, :, g, :]),
                    reads=[vbk], writes=[('v', l, 1 + half * 4 + bi) for bi in range(4)])
        S.op('dve', lambda e: e.tensor_copy(out=uhalo[l][:, :, 0:30], in_=ubf[:, :, T:T + 30]),
             reads=ukeys, writes=[('uhalo', l)])

        S.phase('xnbuf', [('mA', qb) for qb in range(NBLK)])
        mA = xn
        S.op('dve', lambda e: e.tensor_copy(out=gAfull[:, :, :], in_=con[:, cb + O_ATTN:cb + O_ATTN + 4].unsqueeze(2).broadcast_to([128, 4, 128])),
             reads=['con'], writes=['gAfull'])
        pend = {}
        S.phase('MB', [('mB', cc, t) for cc in range(4) for t in range(NTL)])

        def s_stage(qb):
            tq = qb // 4
            kbs = [1] if (first_seq_block and qb == 0) else [0, 1]
            out = []
            for g in range(2):
                pr = slice(g * 64, (g + 1) * 64)
                ets = []
                for kb in kbs:
                    kblk = qb + kb
                    kkey = ('k', l, 'halo') if kblk == 0 else ('k', l, (kblk - 1) // 4)
                    sbk_ap, sbk = bank()
                    rhs_q = qT[:, :, qb * 128:(qb + 1) * 128]
                    mm_group(sbk_ap[:, :], sbk, [
                        (kT[l][:, g, kblk * 128:(kblk + 1) * 128], rhs_q, [kkey] + [('q', c, tq) for c in range(4)]),
                        (identb[:, :], maskneg[:, kb, :], ['ident', 'maskneg'])])
                    ti = rot('et', NET)
                    S.op('act', lambda e, ti=ti, sbk_ap=sbk_ap: e.activation(out=etb[ti][:, :], in_=sbk_ap[:, :], func=AF.Exp, scale=0.125),
                         reads=[sbk], writes=[('et', ti)])
                    ets.append((ti, kblk))
                out.append(ets)
            pend[qb] = out

        def pv_stage(qb):
            ab = attn_blk[qb % 2]
            for g in range(2):
                ets = pend[qb][g]
                pv, pvk = bank()
                mm_group(pv[:, :], pvk, [(vaug[l][:, kblk, g, :], etb[ti][:, :], [('et', ti), ('v', l, kblk), ('vones', l)]) for ti, kblk in ets])
                dn, dnk = gtmp()
                po = slice(g * 64, (g + 1) * 64)
                pd = slice((1 - g) * 64, (2 - g) * 64)
                S.op('dve', lambda e, dn=dn, pv=pv, g=g, po=po, pd=pd: e.tensor_tensor(
                    out=dn[po, :].rearrange("p (h q) -> p h q", h=4),
                    in0=pv[pd, :].rearrange("p (h q) -> p h q", h=4),
                    in1=es[pd, l, g * 4:(g + 1) * 4].unsqueeze(2).broadcast_to([64, 4, 128]), op=ALU.add),
                    reads=[pvk, ('es', l)], writes=[dnk])
                S.op('act', lambda e, dn=dn, po=po: e.activation(out=dn[po, :], in_=dn[po, :], func=AF.Ln), reads=[dnk], writes=[dnk])
                S.op('act', lambda e, dn=dn, po=po: e.activation(out=dn[po, :], in_=dn[po, :], func=AF.Exp, scale=-1.0), reads=[dnk], writes=[dnk])
                S.op('dve', lambda e, dn=dn, pv=pv, g=g, ab=ab, po=po: e.tensor_tensor(
                    out=ab[po, :, :],
                    in0=pv[po, :].rearrange("p (h q) -> p h q", h=4),
                    in1=dn[po, :].rearrange("p (h q) -> p h q", h=4), op=ALU.mult),
                    reads=[pvk, dnk], writes=[('ab', qb % 2, g)])
            abk = [('ab', qb % 2, 0), ('ab', qb % 2, 1)]
            S.op('act', lambda e, ab=ab, qb=qb: e.activation(out=sqA[qb % 2][:, :, :], in_=ab[:, :, :], func=AF.Square),
                 reads=abk, writes=[('sqA', qb % 2)])
            S.op('dve', lambda e, ab=ab, qb=qb: e.tensor_tensor(out=ab[:, :, :], in0=ab[:, :, :], in1=gAfull[:, :, :], op=ALU.mult),
                 reads=abk + ['gAfull'], writes=abk)

        def norm_stage(qb):
            nb, nbk = bank()
            mm_group(nb[:, 0:128], nbk, [(ones[:, :], sqA[qb % 2][:, c, :], [('sqA', qb % 2), 'ones']) for c in range(4)])
            r, rk = grs()
            S.op('act', lambda e, r=r, nb=nb: e.activation(out=r[:, 0:128], in_=nb[:, 0:128], func=AF.Ln, bias=EPS, scale=1.0 / 512.0),
                 reads=[nbk], writes=[rk])
            S.op('act', lambda e, r=r: e.activation(out=r[:, 0:128], in_=r[:, 0:128], func=AF.Exp, scale=-0.5), reads=[rk], writes=[rk])
            S.op('dve', lambda e, r=r, qb=qb: e.tensor_tensor(
                out=mA[:, 0:4, qb * 128:(qb + 1) * 128], in0=attn_blk[qb % 2][:, :, :],
                in1=r[:, 0:128].unsqueeze(1).broadcast_to([128, 4, 128]), op=ALU.mult),
                reads=[('ab', qb % 2, 0), ('ab', qb % 2, 1), rk], writes=[('mA', qb)])

        conv_groups = [(t, cc) for t in range(NTL) for cc in range(4)]
        cstate = {}

        def build_dg(i):
            t, cc = conv_groups[i]
            dg = Dg[cc % 2]
            dgk = ('dg', cc % 2)
            S.op('dve', lambda e, cc=cc, dg=dg: e.tensor_tensor(
                out=dg[:, :, :], in0=identb[:, :].unsqueeze(1).broadcast_to([128, CW, 128]),
                in1=con[:, cb + O_CW + cc * CW:cb + O_CW + (cc + 1) * CW].unsqueeze(2).broadcast_to([128, CW, 128]), op=ALU.mult),
                reads=['ident', 'con'], writes=[dgk])

        def conv_part(i, part):
            t, cc = conv_groups[i]
            dg = Dg[cc % 2]
            dgk = ('dg', cc % 2)
            if part == 0:
                cstate[i] = bank()
            yb, ybk = cstate[i]
            taps = range(0, 16) if part == 0 else range(16, CW)
            mm_group(yb[:, :], ybk, [(dg[:, j, :], ubf[:, cc, j + t * TT:j + (t + 1) * TT], [dgk, ('u', cc)]) for j in taps],
                     first=(part == 0), last=(part == 1))
            if part == 1:
                S.op('act', lambda e, yb=yb, cc=cc, t=t: e.activation(out=yconv[:, cc, ts(t)], in_=yb[:, :], func=AF.Identity, bias=gcol(cb + O_CB, cc)),
                     reads=[ybk, 'con'], writes=[('y', cc, t)])

        lst = {}

        def LA_pre(t):
            yk = [('y', cc, t) for cc in range(4)]
            yv = yconv[:, :, ts(t)]
            S.op('act', lambda e, yv=yv: e.activation(out=sqb[:, 0:4, :], in_=yv, func=AF.Copy), reads=yk, writes=['sqb_lo'])
            S.op('act', lambda e, yv=yv: e.activation(out=sqb[:, 4:8, :], in_=yv, func=AF.Square), reads=yk, writes=['sqb_hi'])

        def LA_pe(t):
            mb, mbk = bank()
            mm_group(mb[:, :], mbk, [(ones[:, :], sqb[:, c, :], ['sqb_lo', 'ones']) for c in range(4)])
            qb_, qbk = bank()
            mm_group(qb_[:, :], qbk, [(ones[:, :], sqb[:, 4 + c, :], ['sqb_hi', 'ones']) for c in range(4)])
            lst[('A', t)] = (mb, mbk, qb_, qbk)

        def LB_pre(t):
            mb, mbk, qb_, qbk = lst[('A', t)]
            yk = [('y', cc, t) for cc in range(4)]
            yv = yconv[:, :, ts(t)]
            mu, muk = gtmp()
            S.op('act', lambda e: e.activation(out=mu[:, :], in_=mb[:, :], func=AF.Copy, scale=1.0 / 512.0), reads=[mbk], writes=[muk])
            m2, m2k = gtmp()
            S.op('act', lambda e: e.activation(out=m2[:, :], in_=mb[:, :], func=AF.Square, scale=1.0 / 512.0), reads=[mbk], writes=[m2k])
            S.op('dve', lambda e: e.scalar_tensor_tensor(out=m2[:, :], in0=qb_[:, :], scalar=1.0 / 512.0, in1=m2[:, :], op0=ALU.mult, op1=ALU.subtract),
                 reads=[qbk, m2k], writes=[m2k])
            S.op('act', lambda e: e.activation(out=m2[:, :], in_=m2[:, :], func=AF.Ln, bias=EPS), reads=[m2k], writes=[m2k])
            S.op('act', lambda e: e.activation(out=m2[:, :], in_=m2[:, :], func=AF.Exp, scale=-0.5), reads=[m2k], writes=[m2k])
            S.op('dve', lambda e: e.tensor_tensor(out=yv, in0=yv, in1=mu[:, :].unsqueeze(1).broadcast_to([128, 4, TT]), op=ALU.subtract),
                 reads=yk + [muk], writes=yk)
            S.op('dve', lambda e: e.tensor_tensor(out=yv, in0=yv, in1=m2[:, :].unsqueeze(1).broadcast_to([128, 4, TT]), op=ALU.mult),
                 reads=yk + [m2k], writes=yk)
            for cc in range(4):
                S.op('dve', lambda e, cc=cc: e.tensor_scalar(
                    out=yconv[:, cc, ts(t)], in0=yconv[:, cc, ts(t)], scalar1=gcol(cb + O_LNG, cc), scalar2=gcol(cb + O_LNB, cc),
                    op0=ALU.mult, op1=ALU.add), reads=[('y', cc, t), 'con'], writes=[('y', cc, t)])

        def LB_pre_b(t):
            yk = [('y', cc, t) for cc in range(4)]
            yv = yconv[:, :, ts(t)]
            S.op('act', lambda e: e.activation(out=yv, in_=yv, func=AF.Silu), reads=yk, writes=yk)
            S.op('act', lambda e: e.activation(out=sqb[:, 0:4, :], in_=yv, func=AF.Square), reads=yk, writes=['sqb_lo'])

        def LB_pe(t):
            rb, rbk = bank()
            mm_group(rb[:, :], rbk, [(ones[:, :], sqb[:, c, :], ['sqb_lo', 'ones']) for c in range(4)])
            lst[('B', t)] = (rb, rbk)

        def LC(t):
            rb, rbk = lst[('B', t)]
            r2, r2k = grs()
            S.op('act', lambda e: e.activation(out=r2[:, :], in_=rb[:, :], func=AF.Ln, bias=EPS, scale=1.0 / 512.0), reads=[rbk], writes=[r2k])
            S.op('act', lambda e: e.activation(out=r2[:, :], in_=r2[:, :], func=AF.Exp, scale=-0.5), reads=[r2k], writes=[r2k])
            for cc in range(4):
                S.op('dve', lambda e, cc=cc: e.scalar_tensor_tensor(
                    out=mixedB[:, cc, ts(t)], in0=yconv[:, cc, ts(t)], scalar=gcol(cb + O_CONVN, cc), in1=r2[:, :], op0=ALU.mult, op1=ALU.mult),
                    reads=[('y', cc, t), r2k, 'con'], writes=[('mB', cc, t)])

        wo_slots = {}
        wo_done = [0]
        wo_order = [(mp, mm, t) for t in range(NTL) for mp in range(4) for mm in range(2)]

        def wo_group(mp, mm, t):
            if mp not in wo_slots:
                wo_slots[mp] = loadA(wslice(wout, l, mp * 256, mp * 256 + 256))
            wa, wak = wo_slots[mp]
            m = mp * 2 + mm
            cs = slice(mm * 128, (mm + 1) * 128)
            ob, obk = bank()
            terms = [(wa[:, c, cs], mA[:, c, ts(t)], [wak] + [('mA', qb) for qb in range(t * 4, t * 4 + 4)]) for c in range(4)]
            terms += [(wa[:, 4 + cc, cs], mixedB[:, cc, ts(t)], [wak, ('mB', cc, t)]) for cc in range(4)]
            mm_group(ob[:, :], obk, terms)
            S.op('dve', lambda e: e.tensor_tensor(out=hT[:, m, ts(t)], in0=ob[:, :], in1=hT[:, m, ts(t)], op=ALU.add),
                 reads=[obk, ('h', m, t)], writes=[('h', m, t)])
            wo_done[0] += 1

        pre_sched = {4: [lambda: LA_pre(0)], 5: [lambda: LB_pre(0)], 7: [lambda: LC(0)], 8: [lambda: LA_pre(1)], 9: [lambda: LB_pre(1)]}
        pe_sched = {4: [lambda: LA_pe(0)], 6: [lambda: LB_pe(0)], 8: [lambda: LA_pe(1)]}
        late_sched = {5: [lambda: LB_pre_b(0)], 9: [lambda: LB_pre_b(1)]}
        build_dg(0)
        for i in range(NBLK + 2):
            if i < NBLK:
                s_stage(i)
            if i + 1 < NBLK:
                build_dg(i + 1)
            for f in pre_sched.get(i, []):
                f()
            if i < NBLK:
                conv_part(i, 0)
            if 1 <= i < NBLK + 1:
                pv_stage(i - 1)
            if i < NBLK:
                conv_part(i, 1)
            for f in late_sched.get(i, []):
                f()
            if 2 <= i < NBLK + 2:
                norm_stage(i - 2)
            if i >= NBLK:
                for mp_, mm_, t_ in wo_order[wo_done[0]:wo_done[0] + 2]:
                    wo_group(mp_, mm_, t_)
            for f in pe_sched.get(i, []):
                f()
        for mp_, mm_, t_ in wo_order[wo_done[0]:NBLK]:
            wo_group(mp_, mm_, t_)
        LB_pe(1)
        LC(1)

        for mp, mm, t in wo_order[wo_done[0]:]:
            wo_group(mp, mm, t)
            if (mp, mm, t) == (0, 0, 1) and tail_hook is not None:
                tail_hook()

    def ple_prefetch(l, st):
        t0 = st * T
        S.phase('MB', ['ppT', 'wpl'])
        S.op('pool', lambda e: e.dma_start(out=ppT[:, :, :], in_=pT[l].rearrange("(kc p) t -> p kc t", p=128)[:, :, t0:t0 + T]),
             writes=['ppT'], dma_sem=sem_pp)
        S.op('pool', lambda e: e.dma_start(out=wpl[:, :, :], in_=wple[l].rearrange("(kc p) n -> p kc n", p=128)),
             writes=['wpl'], dma_sem=sem_wpl)

    def ple(l, st, tail_hook=None):
        cb = l * LW
        t0 = st * T
        S.phase('xnbuf', [('xn', c, t) for c in range(8) for t in range(NTL)])
        S.phase('R1', [('e', m, t) for m in range(8) for t in range(NTL)] + [('sqE', t) for t in range(NTL)])
        sqE = [R1[:, 8192 + t * 2048:8192 + (t + 1) * 2048].bitcast(BF16).rearrange("p (c n) -> p c n", c=8) for t in range(NTL)]
        for t in range(NTL):
            for m in range(8):
                eb, ebk = bank()
                mm_group(eb[:, :], ebk, [(wpl[:, kc, m * 128:(m + 1) * 128], ppT[:, kc, ts(t)], ['wpl', 'ppT']) for kc in range(2)])
                S.op('dve', lambda e, eb=eb, m=m, t=t: e.tensor_copy(out=eT[:, m, ts(t)], in_=eb[:, :]), reads=[ebk], writes=[('e', m, t)])
        for t in range(NTL):
            if t == 0 and early['hb0']:
                early['hb0'] = False
                continue
            S.op('act', lambda e, t=t: e.activation(out=xn[:, :, ts(t)], in_=hT[:, :, ts(t)], func=AF.Copy),
                 reads=hkeys(t), writes=xnkeys(t))
        for t in range(NTL):
            S.op('act', lambda e, t=t: e.activation(out=sqE[t][:, :, :], in_=eT[:, :, ts(t)], func=AF.Square),
                 reads=[('e', m, t) for m in range(8)], writes=[('sqE', t)])
        groups = [(mp, mm, t) for t in range(NTL) for mp in range(4) for mm in range(2)]
        gstate = {}
        wslots = {}

        def gate_pe(i):
            mp, mm, t = groups[i]
            if mp not in wslots:
                wslots[mp] = loadA(wslice(wgate, l, mp * 256, mp * 256 + 256))
            wa, wk = wslots[mp]
            gb, gbk = bank()
            mm_group(gb[:, :], gbk, [(wa[:, kc, mm * 128:(mm + 1) * 128], xn[:, kc, ts(t)], [wk, ('xn', kc, t)]) for kc in range(8)])
            gstate[i] = (gb, gbk)

        def gate_post(i):
            mp, mm, t = groups[i]
            m = mp * 2 + mm
            gb, gbk = gstate[i]
            sg, sgk = gtmp()
            S.op('act', lambda e, sg=sg, gb=gb: e.activation(out=sg[:, :], in_=gb[:, :], func=AF.Sigmoid), reads=[gbk], writes=[sgk])
            S.op('dve', lambda e, sg=sg, m=m, t=t: e.tensor_tensor(out=sg[:, :], in0=sg[:, :], in1=eT[:, m, ts(t)], op=ALU.mult),
                 reads=[sgk, ('e', m, t)], writes=[sgk])
            S.op('dve', lambda e, sg=sg, m=m, t=t: e.tensor_tensor(out=hT[:, m, ts(t)], in0=hT[:, m, ts(t)], in1=sg[:, :], op=ALU.add),
                 reads=[sgk, ('h', m, t)], writes=[('h', m, t)])

        LEAD = 4
        for i in range(LEAD):
            gate_pe(i)
        for t in range(NTL):
            b_, bk_ = bank()
            mm_group(b_[:, :], bk_, [(ones[:, :], sqE[t][:, c, :], [('sqE', t), 'ones']) for c in range(8)])
            r, rk = grs()
            S.op('act', lambda e, r=r, b_=b_: e.activation(out=r[:, :], in_=b_[:, :], func=AF.Ln, bias=EPS, scale=1.0 / float(D)), reads=[bk_], writes=[rk])
            S.op('act', lambda e, r=r: e.activation(out=r[:, :], in_=r[:, :], func=AF.Exp, scale=-0.5), reads=[rk], writes=[rk])
            for m in range(8):
                S.op('dve', lambda e, m=m, t=t, r=r: e.scalar_tensor_tensor(
                    out=eT[:, m, ts(t)], in0=eT[:, m, ts(t)], scalar=gcol(cb + O_PLE, m), in1=r[:, :], op0=ALU.mult, op1=ALU.mult),
                    reads=[('e', m, t), rk, 'con'], writes=[('e', m, t)])
        for i in range(len(groups)):
            if i + LEAD < len(groups):
                gate_pe(i + LEAD)
            gate_post(i)
            if i == 8 and tail_hook is not None:
                tail_hook()

    def final(st):
        t0 = st * T
        okeys = [('o', c, t) for c in range(8) for t in range(NTL)]
        S.phase('R1', okeys)
        for t in range(NTL):
            r, rk = rms_rstd(hT[:, :, ts(t)], hkeys(t), 8, 128, float(D), TT)
            for c in range(8):
                S.op('dve', lambda e, c=c, t=t, r=r: e.scalar_tensor_tensor(
                    out=eT[:, c, ts(t)], in0=hT[:, c, ts(t)], scalar=gcol(O_FINAL, c), in1=r[:, :], op0=ALU.mult, op1=ALU.mult),
                    reads=[('h', c, t), rk, 'con'], writes=[('o', c, t)])
        def store(extra_reads=()):
            S.op('sp', lambda e: e.dma_start(out=outT.rearrange("(c p) t -> p c t", p=128)[:, :, t0:t0 + T], in_=eT[:, :, :]),
                 reads=okeys + list(extra_reads), dma_sem=sem_out)
        return store

    sem_xt = [sem_x, S.new_dma_sem()]

    def load_x(st):
        for t in range(NTL):
            c0 = st * T + t * TT
            S.op('sp', lambda e, c0=c0, t=t: e.dma_start(out=hT[:, :, ts(t)], in_=xT.rearrange("(c p) t -> p c t", p=128)[:, :, c0:c0 + TT]),
                 writes=[('h', c, t) for c in range(8)], dma_sem=sem_xt[t])

    load_x(0)
    for st in range(nst):
        for l in range(nlayer):
            ffn(l, 0, after_norm=(lambda st=st: rope_tables(st)) if l == 0 else None,
                tail_hook=lambda l=l: early_norm(l * LW + O_MIX))
            mixer(l, st, tail_hook=lambda l=l: early_norm(l * LW + O_FFN2))
            ffn(l, 1, after_norm=lambda l=l, st=st: ple_prefetch(l, st), tail_hook=early_hb)
            ple(l, st, tail_hook=(lambda l=l: early_norm((l + 1) * LW + O_FFN1)) if l + 1 < nlayer else None)
        store = final(st)
        if st + 1 < nst:
            load_x(st + 1)
            store(extra_reads=[('h', 0, NTL - 1)])
        else:
            store()
    S.final_wait('sp', [(sem_out, S.dcnt[sem_out])])
    S.emit()
    return nc


def _chunks(v, n, p=128):
    return np.ascontiguousarray(np.asarray(v, np.float32).reshape(n, p).T)


def _host_consts(inp):
    con = np.zeros((128, NCON), np.float32)
    for l in range(NLAYER):
        b = l * LW
        con[:, b + O_FFN1:b + O_FFN1 + 8] = _chunks(inp["ffn1_norm"][l], 8)
        con[:, b + O_MIX:b + O_MIX + 8] = _chunks(inp["mix_norm"][l], 8)
        con[:, b + O_FFN2:b + O_FFN2 + 8] = _chunks(inp["ffn2_norm"][l], 8)
        con[:, b + O_PLE:b + O_PLE + 8] = _chunks(inp["ple_norm"][l], 8)
        a = _chunks(inp["attn_out_norm"][l], 8, 64)
        con[:, b + O_ATTN:b + O_ATTN + 4] = np.concatenate([a[:, 0:4], a[:, 4:8]], axis=0)
        con[:, b + O_CONVN:b + O_CONVN + 4] = _chunks(inp["conv_out_norm"][l], 4)
        con[:, b + O_LNG:b + O_LNG + 4] = _chunks(inp["conv_ln_g"][l], 4)
        con[:, b + O_LNB:b + O_LNB + 4] = _chunks(inp["conv_ln_b"][l], 4)
        con[:, b + O_CB:b + O_CB + 4] = _chunks(inp["conv_b"][l], 4)
        cw = np.asarray(inp["conv_w"][l], np.float32).reshape(CW, 4, 128)
        con[:, b + O_CW:b + O_CW + CW * 4] = cw.transpose(2, 1, 0).reshape(128, CW * 4)
        con[:, b + O_SINK:b + O_SINK + 8] = np.broadcast_to(np.asarray(inp["sinks"][l], np.float32)[None, :], (128, 8))
    con[:, O_FINAL:O_FINAL + 8] = _chunks(inp["final_norm"], 8)
    inv = (500000.0 ** (-np.arange(0, 16, 2, dtype=np.float32) / 16.0)).astype(np.float32)
    for p in range(128):
        d = p % 64
        con[p, O_PHC] = np.float32(np.pi / 2)
        if d < 16:
            con[p, O_FREQ] = inv[d % 8]
            con[p, O_PHS] = np.float32(np.pi) if d < 8 else 0.0
    return con


def _host_mask():
    j = np.arange(128)[:, None]
    i = np.arange(128)[None, :]
    m = np.zeros((128, 2, 128), np.float32)
    m[:, 0, :] = (j > i)
    m[:, 1, :] = (j <= i)
    return m


def _host_win(w_in):
    w = np.asarray(w_in, np.float32)
    L = w.shape[0]

    def headcols(base, h):
        return base + h * 64 + np.arange(64)

    def swapped(cols):
        c = cols.copy()
        c[0:8] = cols[8:16]
        c[8:16] = cols[0:8]
        return c
    idx = []
    for c in range(4):
        q = np.concatenate([headcols(0, c), headcols(0, 4 + c)])
        qs = np.concatenate([swapped(headcols(0, c)), swapped(headcols(0, 4 + c))])
        idx += [q, qs]
    k = np.concatenate([headcols(512, 0), headcols(512, 1)])
    ks = np.concatenate([swapped(headcols(512, 0)), swapped(headcols(512, 1))])
    idx += [k, ks]
    for cc in range(4):
        idx += [768 + cc * 128 + np.arange(128), 1280 + cc * 128 + np.arange(128)]
    idx += [640 + np.arange(128), 640 + np.arange(128)]
    idx = np.concatenate(idx)
    return np.ascontiguousarray(w[:, :, idx])


def _host_wout(w_out):
    w = np.asarray(w_out, np.float32)
    idx = []
    for c in range(4):
        idx += [c * 64 + np.arange(64), (4 + c) * 64 + np.arange(64)]
    idx.append(512 + np.arange(512))
    idx = np.concatenate(idx)
    return np.ascontiguousarray(w[:, idx, :])


def _prep_shared(inp):
    return {
        "wgu1": np.ascontiguousarray(inp["ffn1_w_gate_up"], np.float32),
        "wgu2": np.ascontiguousarray(inp["ffn2_w_gate_up"], np.float32),
        "wd1": np.ascontiguousarray(inp["ffn1_w_down"], np.float32),
        "wd2": np.ascontiguousarray(inp["ffn2_w_down"], np.float32),
        "win": _host_win(inp["w_in"]),
        "wout": _host_wout(inp["w_out"]),
        "wple": np.ascontiguousarray(inp["w_ple"], np.float32),
        "wgate": np.ascontiguousarray(inp["w_ple_gate"], np.float32),
        "con": _host_consts(inp),
        "mask": _host_mask(),
        "ident": np.eye(128, dtype=np.float32),
    }


def run(inp, cores, seq=SEQ):
    nst = seq // T
    nc = build_program(nst=nst, seq=seq)
    shared = _prep_shared(inp)
    in_maps = []
    for b in cores:
        m = dict(shared)
        m["xT"] = np.ascontiguousarray(np.asarray(inp["x"][b, :seq], np.float32).T)
        m["pT"] = np.ascontiguousarray(np.asarray(inp["p"][:, b, :seq], np.float32).transpose(0, 2, 1))
        m["pos"] = np.ascontiguousarray(np.asarray(inp["positions"][b, :seq], np.int32)[None, :])
        in_maps.append(m)
    res = run_bass_kernel_spmd(nc, in_maps, core_ids=list(range(len(cores))))
    outs = [np.ascontiguousarray(r["outT"].T) for r in res.results]
    return np.stack(outs, axis=0)


def kernel(**inputs):
    out = run(inputs, list(range(8)), SEQ)
    return out.astype(np.float32)
```

```python
import numpy as np
import concourse.bass as bass
import concourse.mybir as mybir
from concourse.bass_utils import run_bass_kernel_spmd

F32 = mybir.dt.float32
BF16 = mybir.dt.bfloat16
I32 = mybir.dt.int32
ALU = mybir.AluOpType
AF = mybir.ActivationFunctionType
ENGS = ('pe', 'act', 'dve', 'pool', 'sp')

D = 1024
SEQ = 4096
DFF = 2816
NJ = DFF // 128
T = 1024
TT = 512
NTL = T // TT
NBLK = T // 128
EPS = 1e-6
NLAYER = 2
CW = 31
WIN_CHUNKS = 20

O_FFN1, O_MIX, O_FFN2, O_PLE, O_ATTN, O_CONVN, O_LNG, O_LNB, O_CB, O_CW, O_SINK = 0, 8, 16, 24, 32, 40, 44, 48, 52, 56, 180
LW = 188
O_FINAL = NLAYER * LW
O_FREQ = O_FINAL + 8
O_PHC = O_FREQ + 1
O_PHS = O_FREQ + 2
NCON = O_FREQ + 4


class Sched:
    def __init__(s, nc, same_engine_sync=True):
        s.nc = nc
        s.streams = {e: [] for e in ENGS}
        s.esem = {e: nc.alloc_semaphore(name=f"sem_{e}") for e in ENGS if e != 'sp'}
        s.ecnt = {e: 0 for e in ENGS}
        s.obs = {e: {} for e in ENGS}
        s.lastw = {}
        s.readers = {}
        s.dcnt = {}
        s.same = same_engine_sync
        s.nsem = 0
        s.phase_keys = {}

    def new_dma_sem(s, name=None):
        s.nsem += 1
        h = s.nc.alloc_semaphore(name=name or f"dsem{s.nsem}")
        s.dcnt[h] = 0
        return h

    def op(s, eng, fn, reads=(), writes=(), inc=True, dma_sem=None):
        waits = {}
        own = s.esem.get(eng)

        def need(tok):
            if tok is None:
                return
            sem, val = tok
            if sem is own and (eng == 'pe' or not s.same):
                return
            if s.obs[eng].get(sem, 0) >= val:
                return
            if waits.get(sem, (None, 0))[1] < val:
                waits[sem] = (sem, val)
        for k in reads:
            need(s.lastw.get(k))
        for k in writes:
            need(s.lastw.get(k))
            for t in s.readers.get(k, ()):
                need(t)
        for sem, val in waits.values():
            s.obs[eng][sem] = val
        if dma_sem is not None:
            s.dcnt[dma_sem] += 16
            tok = (dma_sem, s.dcnt[dma_sem])
            incspec = (dma_sem, 16)
        elif inc:
            s.ecnt[eng] += 1
            tok = (own, s.ecnt[eng])
            incspec = (own, 1)
        else:
            tok = (own, s.ecnt[eng] + 1)
            incspec = None
        s.streams[eng].append((list(waits.values()), fn, incspec))
        for k in reads:
            s.readers.setdefault(k, []).append(tok)
        for k in writes:
            s.lastw[k] = tok
            s.readers[k] = []
        return tok

    def phase(s, buf, new_keys):
        old = s.phase_keys.get(buf, [])
        toks = []
        for k in old:
            if s.lastw.get(k) is not None:
                toks.append(s.lastw[k])
            toks.extend(s.readers.get(k, ()))
        best = {}
        for sem, val in toks:
            if best.get(sem, (None, 0))[1] < val:
                best[sem] = (sem, val)
        toks = list(best.values())
        for k in new_keys:
            s.readers.setdefault(k, []).extend(toks)
        s.phase_keys[buf] = list(new_keys)

    def final_wait(s, eng, toks):
        waits = {}
        for sem, val in toks:
            if waits.get(sem, (None, 0))[1] < val:
                waits[sem] = (sem, val)
        s.streams[eng].append((list(waits.values()), None, None))

    def emit(s):
        nc = s.nc
        with nc.Block() as block:
            def run(engname):
                def body(e):
                    for waits, fn, incspec in s.streams[engname]:
                        for sem, val in waits:
                            e.wait_ge(sem, val)
                        if fn is None:
                            continue
                        ins = fn(e)
                        if incspec is not None:
                            ins.then_inc(incspec[0], incspec[1])
                return body
            block.tensor(run('pe'))
            block.scalar(run('act'))
            block.vector(run('dve'))
            block.gpsimd(run('pool'))
            block.sync(run('sp'))


def build_program(nst=SEQ // T, nlayer=NLAYER, seq=SEQ):
    nc = bass.Bass("TRN2", target_bir_lowering=False)

    def din(name, shape, dt=F32):
        return nc.dram_tensor(name, shape, dt, kind="ExternalInput").ap()
    xT = din("xT", [D, seq])
    pT = din("pT", [NLAYER, 256, seq])
    pos = din("pos", [1, seq], I32)
    wgu = [din("wgu1", [NLAYER, D, 2 * DFF]), din("wgu2", [NLAYER, D, 2 * DFF])]
    wdn = [din("wd1", [NLAYER, DFF, D]), din("wd2", [NLAYER, DFF, D])]
    win = din("win", [NLAYER, D, WIN_CHUNKS * 128])
    wout = din("wout", [NLAYER, D, D])
    wple = din("wple", [NLAYER, 256, D])
    wgate = din("wgate", [NLAYER, D, D])
    con_d = din("con", [128, NCON])
    mask_d = din("mask", [128, 2, 128])
    ident_d = din("ident", [128, 128])
    outT = nc.dram_tensor("outT", [D, seq], F32, kind="ExternalOutput").ap()

    S = Sched(nc)
    sb = nc.alloc_sbuf_tensor
    hT = sb("hT", [128, 8, T], F32)
    xn = sb("xn", [128, 8, T], BF16)
    R1 = sb("R1", [128, 12288], F32)
    hid = R1[:, 0:11264].bitcast(BF16).rearrange("p (j t) -> p j t", j=NJ)
    UW = 1056
    ubf = R1[:, 0:2112].bitcast(BF16).rearrange("p (c t) -> p c t", c=4)
    yconv = R1[:, 2112:2112 + 4 * T].rearrange("p (c t) -> p c t", c=4)
    qT = R1[:, 6208:6208 + 2 * T].bitcast(BF16).rearrange("p (c t) -> p c t", c=4)
    Dg = [R1[:, 8256 + i * 1984:8256 + (i + 1) * 1984].bitcast(BF16).rearrange("p (j m) -> p j m", j=CW) for i in range(2)]
    eT = R1[:, 0:8 * T].rearrange("p (c t) -> p c t", c=8)
    MB = sb("MB", [128, 4 * T], BF16)
    mixedB = MB[:, :].rearrange("p (c t) -> p c t", c=4)
    wpl = MB[:, 0:2 * D].rearrange("p (k n) -> p k n", k=2)
    ppT = MB[:, 2 * D:2 * D + 2 * T].rearrange("p (k t) -> p k t", k=2)
    kT = [sb(f"kT{l}", [128, 2, 128 + T], BF16) for l in range(nlayer)]
    vaug = [sb(f"vaug{l}", [128, NBLK + 1, 2, 128], BF16) for l in range(nlayer)]
    uhalo = [sb(f"uhalo{l}", [128, 4, 32], BF16) for l in range(nlayer)]
    cosT = sb("cosT", [128, T], F32)
    sinT = sb("sinT", [128, T], F32)
    sqb = sb("sqb", [128, 8, TT], BF16)
    NRS = 2
    rs = [sb(f"rs{i}", [128, TT], F32) for i in range(NRS)]
    NTMP = 4
    tmp = [sb(f"tmp{i}", [128, TT], F32) for i in range(NTMP)]
    maskneg = sb("maskneg", [128, 2, TT], BF16)
    NET = 8
    etb = [sb(f"et{i}", [128, TT], BF16) for i in range(NET)]
    attn_blk = [sb(f"attn_blk{i}", [128, 4, 128], F32) for i in range(2)]
    sqA = [sb(f"sqA{i}", [128, 4, 128], BF16) for i in range(2)]
    gAfull = sb("gAfull", [128, 4, 128], F32)
    identb = sb("identb", [128, 128], BF16)
    con = sb("con_sb", [128, NCON], F32)
    ones = sb("ones", [128, 128], BF16)
    maskb = sb("mask_sb", [128, 2, 128], BF16)
    es = sb("es", [128, NLAYER, 8], F32)
    NA = 5
    Asl = [sb(f"Asl{i}", [128, 8, 256], BF16) for i in range(NA)]
    NB = 3
    Bsl = [sb(f"Bsl{i}", [128, NJ, 128], BF16) for i in range(NB)]
    NPS = 8
    ps = [nc.alloc_psum_tensor(f"ps{i}", [128, TT], F32) for i in range(NPS)]

    A_sem = [S.new_dma_sem() for _ in range(NA)]
    B_sem = [S.new_dma_sem() for _ in range(NB)]
    sem_wpl = S.new_dma_sem()
    sem_pp = S.new_dma_sem()
    sem_x = S.new_dma_sem()
    sem_c = S.new_dma_sem()
    sem_pos = [S.new_dma_sem() for _ in range(NTL)]
    sem_m = S.new_dma_sem()
    sem_i = S.new_dma_sem()
    sem_out = S.new_dma_sem()

    st8 = {'ps': 0, 'tmp': 0, 'rs': 0, 'A': 0, 'B': 0, 'et': 0}

    def rot(kind, n):
        i = st8[kind]
        st8[kind] = (i + 1) % n
        return i

    def bank():
        i = rot('ps', NPS)
        return ps[i], ('ps', i)

    def gtmp():
        i = rot('tmp', NTMP)
        return tmp[i], ('tmp', i)

    def grs():
        i = rot('rs', NRS)
        return rs[i], ('rs', i)

    def ts(t):
        return slice(t * TT, (t + 1) * TT)

    def mm_group(out_ap, out_key, terms, first=True, last=True):
        n = len(terms)
        for i, (l, r, rk) in enumerate(terms):
            st_ = first and i == 0
            sp_ = last and i == n - 1
            S.op('pe', lambda e, l=l, r=r, st_=st_, sp_=sp_: e.matmul(out_ap, lhsT=l, rhs=r, start=st_, stop=sp_),
                 reads=rk, writes=[out_key], inc=sp_)

    def loadA(src_ap, view=None):
        i = rot('A', NA)
        dst = Asl[i][:, :, :] if view is None else view(Asl[i])
        S.op('pool', lambda e: e.dma_start(out=dst, in_=src_ap), writes=[('A', i)], dma_sem=A_sem[i])
        return Asl[i], ('A', i)

    def loadB(src_ap):
        i = rot('B', NB)
        S.op('pool', lambda e: e.dma_start(out=Bsl[i][:, :, :], in_=src_ap), writes=[('B', i)], dma_sem=B_sem[i])
        return Bsl[i], ('B', i)

    def gcol(off, c=None):
        if c is None:
            return con[:, off:off + 1]
        return con[:, off + c:off + c + 1]

    S.op('sp', lambda e: e.dma_start(out=con[:, :], in_=con_d), writes=['con'], dma_sem=sem_c)
    S.op('pool', lambda e: e.dma_start(out=maskb[:, :, :], in_=mask_d), writes=['mask'], dma_sem=sem_m)
    S.op('dve', lambda e: e.memset(ones[:, :], 1.0), writes=['ones'])
    for kb in range(2):
        S.op('dve', lambda e, kb=kb: e.tensor_scalar(
            out=maskneg[:, kb, :].rearrange("p (h q) -> p h q", h=4), in0=maskb[:, kb, :].unsqueeze(1).broadcast_to([128, 4, 128]),
            scalar1=-1.0, scalar2=30000.0, op0=ALU.add, op1=ALU.mult), reads=['mask'], writes=['maskneg'])
    S.op('pool', lambda e: e.dma_start(out=identb[:, :], in_=ident_d), writes=['ident'], dma_sem=sem_i)
    for l in range(nlayer):
        S.op('act', lambda e, l=l: e.activation(out=es[:, l, :], in_=con[:, l * LW + O_SINK:l * LW + O_SINK + 8], func=AF.Exp),
             reads=['con'], writes=[('es', l)])
        S.op('dve', lambda e, l=l: e.memset(vaug[l][:, :, 0, 64:128], 1.0), writes=[('vones', l)])
        S.op('dve', lambda e, l=l: e.memset(vaug[l][:, :, 1, 0:64], 1.0), writes=[('vones', l)])
        S.op('dve', lambda e, l=l: e.memset(uhalo[l][:, :, :], 0.0), writes=[('uhalo', l)])
        S.op('dve', lambda e, l=l: e.memset(kT[l][:, :, :], 0.0), writes=[('k', l, 'halo'), ('k', l, 0), ('k', l, 1)])

    def rms_rstd(src3, src_keys, nchunk, kparts, denom, ncols):
        sq_view = sqb[0:kparts, 0:nchunk, 0:ncols]
        S.op('act', lambda e: e.activation(out=sq_view, in_=src3, func=AF.Square), reads=src_keys, writes=['sqb_lo', 'sqb_hi'])
        b, bk = bank()
        mm_group(b[:, 0:ncols], bk, [(ones[0:kparts, :], sqb[0:kparts, c, 0:ncols], ['sqb_lo', 'sqb_hi', 'ones']) for c in range(nchunk)])
        r, rk = grs()
        S.op('act', lambda e: e.activation(out=r[:, 0:ncols], in_=b[:, 0:ncols], func=AF.Ln, bias=EPS, scale=1.0 / denom),
             reads=[bk], writes=[rk])
        S.op('act', lambda e: e.activation(out=r[:, 0:ncols], in_=r[:, 0:ncols], func=AF.Exp, scale=-0.5), reads=[rk], writes=[rk])
        return r, rk

    def hkeys(t):
        return [('h', c, t) for c in range(8)]

    def xnkeys(t):
        return [('xn', c, t) for c in range(8)]

    early = {'skip0': False, 'hb0': False}
    XNKEYS = [('xn', c, t) for c in range(8) for t in range(NTL)]

    def norm_tile(goff, t):
        r, rk = rms_rstd(hT[:, :, ts(t)], hkeys(t), 8, 128, float(D), TT)
        for c in range(8):
            S.op('dve', lambda e, c=c, t=t, r=r: e.scalar_tensor_tensor(
                out=xn[:, c, ts(t)], in0=hT[:, c, ts(t)], scalar=gcol(goff, c), in1=r[:, :], op0=ALU.mult, op1=ALU.mult),
                reads=[('h', c, t), rk, 'con'], writes=[('xn', c, t)])

    def norm_to_xn(goff):
        if early['skip0']:
            early['skip0'] = False
            early['t1'] = lambda: norm_tile(goff, 1)
            return
        for t in range(NTL):
            norm_tile(goff, t)

    def flush_t1():
        f = early.get('t1')
        if f is not None:
            early['t1'] = None
            f()

    def early_norm(goff):
        S.phase('xnbuf', XNKEYS)
        norm_tile(goff, 0)
        early['skip0'] = True

    def early_hb():
        S.phase('xnbuf', XNKEYS)
        S.op('act', lambda e: e.activation(out=xn[:, :, ts(0)], in_=hT[:, :, ts(0)], func=AF.Copy), reads=hkeys(0), writes=xnkeys(0))
        early['hb0'] = True

    def wslice(w3, l, c0, c1):
        return w3[l].rearrange("(kc p) n -> p kc n", p=128)[:, :, c0:c1]

    def ffn(l, which, after_norm=None, tail_hook=None):
        goff = l * LW + (O_FFN1 if which == 0 else O_FFN2)
        S.phase('xnbuf', [('xn', c, t) for c in range(8) for t in range(NTL)])
        norm_to_xn(goff)
        S.phase('R1', [('hid', j, t) for j in range(NJ) for t in range(NTL)])
        W = wgu[which]
        uslots = {}

        def up_group(jp, jj, t):
            if jp not in uslots:
                uslots[jp] = (loadA(wslice(W, l, jp * 256, jp * 256 + 256)), loadA(wslice(W, l, DFF + jp * 256, DFF + jp * 256 + 256)))
            (ga, gk), (ua, uk) = uslots[jp]
            j = 2 * jp + jj
            gb, gbk = bank()
            ub, ubk = bank()
            mm_group(gb[:, :], gbk, [(ga[:, kc, jj * 128:(jj + 1) * 128], xn[:, kc, ts(t)], [gk, ('xn', kc, t)]) for kc in range(8)])
            mm_group(ub[:, :], ubk, [(ua[:, kc, jj * 128:(jj + 1) * 128], xn[:, kc, ts(t)], [uk, ('xn', kc, t)]) for kc in range(8)])
            sg, sgk = gtmp()
            S.op('act', lambda e: e.activation(out=sg[:, :], in_=gb[:, :], func=AF.Silu), reads=[gbk], writes=[sgk])
            S.op('dve', lambda e: e.tensor_tensor(out=hid[:, j, ts(t)], in0=ub[:, :], in1=sg[:, :], op=ALU.mult),
                 reads=[ubk, sgk], writes=[('hid', j, t)])

        up_order = [(jp, jj, 0) for jp in range(2) for jj in range(2)] + [(jp, jj, 1) for jp in range(2) for jj in range(2)]
        up_order += [(jp, jj, t) for jp in range(2, NJ // 2) for jj in range(2) for t in range(NTL)]
        for ui, (jp, jj, t) in enumerate(up_order):
            if ui == 4:
                flush_t1()
            up_group(jp, jj, t)
        if after_norm is not None:
            after_norm()
        Wd = wdn[which]
        order = [(m, t) for m in range(6) for t in range(NTL)] + [(6, 0), (7, 0), (6, 1), (7, 1)]
        bslots = {}
        for oi, (m, t) in enumerate(order):
            if m not in bslots:
                bslots[m] = loadB(Wd[l].rearrange("(jc p) n -> p jc n", p=128)[:, :, m * 128:(m + 1) * 128])
            ba, bk_ = bslots[m]
            ob, obk = bank()
            mm_group(ob[:, :], obk, [(ba[:, jc, :], hid[:, jc, ts(t)], [bk_, ('hid', jc, t)]) for jc in range(NJ)])
            S.op('dve', lambda e, ob=ob, m=m, t=t: e.scalar_tensor_tensor(
                out=hT[:, m, ts(t)], in0=ob[:, :], scalar=0.5, in1=hT[:, m, ts(t)], op0=ALU.mult, op1=ALU.add),
                reads=[obk, ('h', m, t)], writes=[('h', m, t)])
            if oi == 14 and tail_hook is not None:
                tail_hook()

    def rope_tables(st):
        t0 = st * T
        TWO_PI = float(2 * np.pi)
        for t in range(NTL):
            pi_t, pik = gtmp()
            pi_i = pi_t[:, :].bitcast(I32)
            S.op('sp', lambda e, pi_i=pi_i, t=t: e.dma_start(out=pi_i, in_=pos[:, t0 + t * TT:t0 + (t + 1) * TT].partition_broadcast(128)),
                 writes=[pik], dma_sem=sem_pos[t])
            pf, pfk = gtmp()
            S.op('dve', lambda e, pf=pf, pi_i=pi_i: e.tensor_copy(out=pf[:, :], in_=pi_i), reads=[pik], writes=[pfk])
            for tab, tk, phoff in ((cosT, 'cos', O_PHC), (sinT, 'sin', O_PHS)):
                dst = tab[:, ts(t)]
                key = (tk, t)
                a, ak = gtmp()
                ai = a[:, :].bitcast(I32)
                S.op('dve', lambda e, dst=dst, pf=pf, phoff=phoff: e.tensor_scalar(
                    out=dst, in0=pf[:, :], scalar1=gcol(O_FREQ), scalar2=gcol(phoff), op0=ALU.mult, op1=ALU.add),
                    reads=[pfk, 'con'], writes=[key])
                kf, kfk = gtmp()
                S.op('dve', lambda e, kf=kf, dst=dst: e.tensor_scalar(out=kf[:, :], in0=dst, scalar1=float(1.0 / TWO_PI), scalar2=None, op0=ALU.mult),
                     reads=[key], writes=[kfk])
                S.op('dve', lambda e, ai=ai, kf=kf: e.tensor_copy(out=ai, in_=kf[:, :]), reads=[kfk], writes=[ak])
                S.op('dve', lambda e, ai=ai, kf=kf: e.tensor_copy(out=kf[:, :], in_=ai), reads=[ak], writes=[kfk])
                S.op('dve', lambda e, dst=dst, kf=kf: e.scalar_tensor_tensor(out=dst, in0=kf[:, :], scalar=-TWO_PI, in1=dst, op0=ALU.mult, op1=ALU.add),
                     reads=[kfk, key], writes=[key])
                S.op('dve', lambda e, dst=dst, kf=kf: e.tensor_scalar(out=kf[:, :], in0=dst, scalar1=float(np.pi), scalar2=-TWO_PI, op0=ALU.is_gt, op1=ALU.mult),
                     reads=[key], writes=[kfk])
                S.op('dve', lambda e, dst=dst, kf=kf: e.tensor_tensor(out=dst, in0=dst, in1=kf[:, :], op=ALU.add), reads=[key, kfk], writes=[key])
                S.op('dve', lambda e, dst=dst, kf=kf: e.tensor_scalar(out=kf[:, :], in0=dst, scalar1=float(-np.pi), scalar2=TWO_PI, op0=ALU.is_lt, op1=ALU.mult),
                     reads=[key], writes=[kfk])
                S.op('dve', lambda e, dst=dst, kf=kf: e.tensor_tensor(out=dst, in0=dst, in1=kf[:, :], op=ALU.add), reads=[key, kfk], writes=[key])
                S.op('act', lambda e, dst=dst: e.activation(out=dst, in_=dst, func=AF.Sin), reads=[key], writes=[key])

    def mixer(l, st, tail_hook=None):
        cb = l * LW
        first_seq_block = (st == 0)
        S.phase('xnbuf', [('xn', c, t) for c in range(8) for t in range(NTL)])
        norm_to_xn(cb + O_MIX)
        ukeys = [('u', cc) for cc in range(4)]
        ykeys = [('y', cc, t) for cc in range(4) for t in range(NTL)]
        qkeys = [('q', c, t) for c in range(4) for t in range(NTL)]
        S.phase('R1', ukeys + ykeys + qkeys + [('dg', 0), ('dg', 1)])
        if st > 0:
            S.op('dve', lambda e: e.tensor_copy(out=kT[l][:, :, 0:128], in_=kT[l][:, :, T:T + 128]),
                 reads=[('k', l, NTL - 1)], writes=[('k', l, 'halo')])
            S.op('dve', lambda e: e.tensor_copy(out=vaug[l][:, 0, :, :], in_=vaug[l][:, NBLK, :, :]),
                 reads=[('v', l, NBLK), ('vones', l)], writes=[('v', l, 0)])
        for cc in range(4):
            S.op('dve', lambda e, cc=cc: e.tensor_copy(out=ubf[:, cc, 0:30], in_=uhalo[l][:, cc, 0:30]),
                 reads=[('uhalo', l)], writes=[('u', cc)])

        def wl(c0, n):
            return wslice(win, l, c0 * 128, (c0 + n) * 128)
        qslots = {}

        def qk_group(c, t):
            if c not in qslots:
                qslots[c] = loadA(wl(2 * c, 2))
            wa, wk = qslots[c]
            qb_, qbk = bank()
            sb_, sbk = bank()
            mm_group(qb_[:, :], qbk, [(wa[:, kc, 0:128], xn[:, kc, ts(t)], [wk, ('xn', kc, t)]) for kc in range(8)])
            mm_group(sb_[:, :], sbk, [(wa[:, kc, 128:256], xn[:, kc, ts(t)], [wk, ('xn', kc, t)]) for kc in range(8)])
            t1, t1k = gtmp()
            t2, t2k = gtmp()
            S.op('dve', lambda e, t1=t1, qb_=qb_, t=t: e.tensor_tensor(out=t1[:, :], in0=qb_[:, :], in1=cosT[:, ts(t)], op=ALU.mult),
                 reads=[qbk, ('cos', t)], writes=[t1k])
            S.op('dve', lambda e, t2=t2, sb_=sb_, t=t: e.tensor_tensor(out=t2[:, :], in0=sb_[:, :], in1=sinT[:, ts(t)], op=ALU.mult),
                 reads=[sbk, ('sin', t)], writes=[t2k])
            if c < 4:
                S.op('dve', lambda e, c=c, t=t, t1=t1, t2=t2: e.tensor_tensor(out=qT[:, c, ts(t)], in0=t1[:, :], in1=t2[:, :], op=ALU.add),
                     reads=[t1k, t2k], writes=[('q', c, t)])
            else:
                for g in range(2):
                    pr = slice(g * 64, (g + 1) * 64)
                    S.op('dve', lambda e, t=t, t1=t1, t2=t2, g=g, pr=pr: e.tensor_tensor(
                        out=kT[l][pr, g, 128 + t * TT:128 + (t + 1) * TT], in0=t1[pr, :], in1=t2[pr, :], op=ALU.add),
                        reads=[t1k, t2k], writes=[('k', l, t)])

        qk_order = [(0, 0), (1, 0)]
        for c, t in qk_order:
            qk_group(c, t)
        flush_t1()
        for c, t in [(0, 1), (1, 1)] + [(c, t) for c in range(2, 5) for t in range(NTL)]:
            qk_group(c, t)
        for cc in range(4):
            wa, wk = loadA(wl(10 + 2 * cc, 2))
            for t in range(NTL):
                ab, abk = bank()
                gb, gbk = bank()
                mm_group(ab[:, :], abk, [(wa[:, kc, 0:128], xn[:, kc, ts(t)], [wk, ('xn', kc, t)]) for kc in range(8)])
                mm_group(gb[:, :], gbk, [(wa[:, kc, 128:256], xn[:, kc, ts(t)], [wk, ('xn', kc, t)]) for kc in range(8)])
                sg, sgk = gtmp()
                S.op('act', lambda e, sg=sg, gb=gb: e.activation(out=sg[:, :], in_=gb[:, :], func=AF.Sigmoid), reads=[gbk], writes=[sgk])
                S.op('dve', lambda e, sg=sg, ab=ab, cc=cc, t=t: e.tensor_tensor(out=ubf[:, cc, 30 + t * TT:30 + (t + 1) * TT], in0=ab[:, :], in1=sg[:, :], op=ALU.mult),
                     reads=[abk, sgk], writes=[('u', cc)])
        wa, wk = loadA(wl(18, 2))
        for half in range(2):
            vb, vbk = bank()
            for bi in range(4):
                blk = half * 4 + bi
                t = blk // 4
                mm_group(vb[:, bi * 128:(bi + 1) * 128], vbk,
                         [(xn[:, kc, blk * 128:(blk + 1) * 128], wa[:, kc, 0:128], [wk, ('xn', kc, t)]) for kc in range(8)])
            for g in range(2):
                S.op('dve', lambda e, vb=vb, half=half, g=g: e.tensor_copy(
                    out=vaug[l][:, 1 + half * 4:1 + half * 4 + 4, g, g * 64:(g + 1) * 64],
                    in_=vb[:, :].rearrange("p (b g d) -> p b g d", b=4, g=2)[:, :, g, :]),
                    reads=[vbk], writes=[('v', l, 1 + half * 4 + bi) for bi in range(4)])
        S.op('dve', lambda e: e.tensor_copy(out=uhalo[l][:, :, 0:30], in_=ubf[:, :, T:T + 30]),
             reads=ukeys, writes=[('uhalo', l)])

        S.phase('xnbuf', [('mA', qb) for qb in range(NBLK)])
        mA = xn
        S.op('dve', lambda e: e.tensor_copy(out=gAfull[:, :, :], in_=con[:, cb + O_ATTN:cb + O_ATTN + 4].unsqueeze(2).broadcast_to([128, 4, 128])),
             reads=['con'], writes=['gAfull'])
        pend = {}
        S.phase('MB', [('mB', cc, t) for cc in range(4) for t in range(NTL)])

        def s_stage(qb):
            tq = qb // 4
            kbs = [1] if (first_seq_block and qb == 0) else [0, 1]
            out = []
            for g in range(2):
                pr = slice(g * 64, (g + 1) * 64)
                ets = []
                for kb in kbs:
                    kblk = qb + kb
                    kkey = ('k', l, 'halo') if kblk == 0 else ('k', l, (kblk - 1) // 4)
                    sbk_ap, sbk = bank()
                    rhs_q = qT[:, :, qb * 128:(qb + 1) * 128]
                    mm_group(sbk_ap[:, :], sbk, [
                        (kT[l][:, g, kblk * 128:(kblk + 1) * 128], rhs_q, [kkey] + [('q', c, tq) for c in range(4)]),
                        (identb[:, :], maskneg[:, kb, :], ['ident', 'maskneg'])])
                    ti = rot('et', NET)
                    S.op('act', lambda e, ti=ti, sbk_ap=sbk_ap: e.activation(out=etb[ti][:, :], in_=sbk_ap[:, :], func=AF.Exp, scale=0.125),
                         reads=[sbk], writes=[('et', ti)])
                    ets.append((ti, kblk))
                out.append(ets)
            pend[qb] = out

        def pv_stage(qb):
            ab = attn_blk[qb % 2]
            for g in range(2):
                ets = pend[qb][g]
                pv, pvk = bank()
                mm_group(pv[:, :], pvk, [(vaug[l][:, kblk, g, :], etb[ti][:, :], [('et', ti), ('v', l, kblk), ('vones', l)]) for ti, kblk in ets])
                dn, dnk = gtmp()
                po = slice(g * 64, (g + 1) * 64)
                pd = slice((1 - g) * 64, (2 - g) * 64)
                S.op('dve', lambda e, dn=dn, pv=pv, g=g, po=po, pd=pd: e.tensor_tensor(
                    out=dn[po, :].rearrange("p (h q) -> p h q", h=4),
                    in0=pv[pd, :].rearrange("p (h q) -> p h q", h=4),
                    in1=es[pd, l, g * 4:(g + 1) * 4].unsqueeze(2).broadcast_to([64, 4, 128]), op=ALU.add),
                    reads=[pvk, ('es', l)], writes=[dnk])
                S.op('act', lambda e, dn=dn, po=po: e.activation(out=dn[po, :], in_=dn[po, :], func=AF.Ln), reads=[dnk], writes=[dnk])
                S.op('act', lambda e, dn=dn, po=po: e.activation(out=dn[po, :], in_=dn[po, :], func=AF.Exp, scale=-1.0), reads=[dnk], writes=[dnk])
                S.op('dve', lambda e, dn=dn, pv=pv, g=g, ab=ab, po=po: e.tensor_tensor(
                    out=ab[po, :, :],
                    in0=pv[po, :].rearrange("p (h q) -> p h q", h=4),
                    in1=dn[po, :].rearrange("p (h q) -> p h q", h=4), op=ALU.mult),
                    reads=[pvk, dnk], writes=[('ab', qb % 2, g)])
            abk = [('ab', qb % 2, 0), ('ab', qb % 2, 1)]
            S.op('act', lambda e, ab=ab, qb=qb: e.activation(out=sqA[qb % 2][:, :, :], in_=ab[:, :, :], func=AF.Square),
                 reads=abk, writes=[('sqA', qb % 2)])
            S.op('dve', lambda e, ab=ab, qb=qb: e.tensor_tensor(out=ab[:, :, :], in0=ab[:, :, :], in1=gAfull[:, :, :], op=ALU.mult),
                 reads=abk + ['gAfull'], writes=abk)

        def norm_stage(qb):
            nb, nbk = bank()
            mm_group(nb[:, 0:128], nbk, [(ones[:, :], sqA[qb % 2][:, c, :], [('sqA', qb % 2), 'ones']) for c in range(4)])
            r, rk = grs()
            S.op('act', lambda e, r=r, nb=nb: e.activation(out=r[:, 0:128], in_=nb[:, 0:128], func=AF.Ln, bias=EPS, scale=1.0 / 512.0),
                 reads=[nbk], writes=[rk])
            S.op('act', lambda e, r=r: e.activation(out=r[:, 0:128], in_=r[:, 0:128], func=AF.Exp, scale=-0.5), reads=[rk], writes=[rk])
            S.op('dve', lambda e, r=r, qb=qb: e.tensor_tensor(
                out=mA[:, 0:4, qb * 128:(qb + 1) * 128], in0=attn_blk[qb % 2][:, :, :],
                in1=r[:, 0:128].unsqueeze(1).broadcast_to([128, 4, 128]), op=ALU.mult),
                reads=[('ab', qb % 2, 0), ('ab', qb % 2, 1), rk], writes=[('mA', qb)])

        conv_groups = [(t, cc) for t in range(NTL) for cc in range(4)]
        cstate = {}

        def build_dg(i):
            t, cc = conv_groups[i]
            dg = Dg[cc % 2]
            dgk = ('dg', cc % 2)
            S.op('dve', lambda e, cc=cc, dg=dg: e.tensor_tensor(
                out=dg[:, :, :], in0=identb[:, :].unsqueeze(1).broadcast_to([128, CW, 128]),
                in1=con[:, cb + O_CW + cc * CW:cb + O_CW + (cc + 1) * CW].unsqueeze(2).broadcast_to([128, CW, 128]), op=ALU.mult),
                reads=['ident', 'con'], writes=[dgk])

        def conv_part(i, part):
            t, cc = conv_groups[i]
            dg = Dg[cc % 2]
            dgk = ('dg', cc % 2)
            if part == 0:
                cstate[i] = bank()
            yb, ybk = cstate[i]
            taps = range(0, 16) if part == 0 else range(16, CW)
            mm_group(yb[:, :], ybk, [(dg[:, j, :], ubf[:, cc, j + t * TT:j + (t + 1) * TT], [dgk, ('u', cc)]) for j in taps],
                     first=(part == 0), last=(part == 1))
            if part == 1:
                S.op('act', lambda e, yb=yb, cc=cc, t=t: e.activation(out=yconv[:, cc, ts(t)], in_=yb[:, :], func=AF.Identity, bias=gcol(cb + O_CB, cc)),
                     reads=[ybk, 'con'], writes=[('y', cc, t)])

        lst = {}

        def LA_pre(t):
            yk = [('y', cc, t) for cc in range(4)]
            yv = yconv[:, :, ts(t)]
            S.op('act', lambda e, yv=yv: e.activation(out=sqb[:, 0:4, :], in_=yv, func=AF.Copy), reads=yk, writes=['sqb_lo'])
            S.op('act', lambda e, yv=yv: e.activation(out=sqb[:, 4:8, :], in_=yv, func=AF.Square), reads=yk, writes=['sqb_hi'])

        def LA_pe(t):
            mb, mbk = bank()
            mm_group(mb[:, :], mbk, [(ones[:, :], sqb[:, c, :], ['sqb_lo', 'ones']) for c in range(4)])
            qb_, qbk = bank()
            mm_group(qb_[:, :], qbk, [(ones[:, :], sqb[:, 4 + c, :], ['sqb_hi', 'ones']) for c in range(4)])
            lst[('A', t)] = (mb, mbk, qb_, qbk)

        def LB_pre(t):
            mb, mbk, qb_, qbk = lst[('A', t)]
            yk = [('y', cc, t) for cc in range(4)]
            yv = yconv[:, :, ts(t)]
            mu, muk = gtmp()
            S.op('act', lambda e: e.activation(out=mu[:, :], in_=mb[:, :], func=AF.Copy, scale=1.0 / 512.0), reads=[mbk], writes=[muk])
            m2, m2k = gtmp()
            S.op('act', lambda e: e.activation(out=m2[:, :], in_=mb[:, :], func=AF.Square, scale=1.0 / 512.0), reads=[mbk], writes=[m2k])
            S.op('dve', lambda e: e.scalar_tensor_tensor(out=m2[:, :], in0=qb_[:, :], scalar=1.0 / 512.0, in1=m2[:, :], op0=ALU.mult, op1=ALU.subtract),
                 reads=[qbk, m2k], writes=[m2k])
            S.op('act', lambda e: e.activation(out=m2[:, :], in_=m2[:, :], func=AF.Ln, bias=EPS), reads=[m2k], writes=[m2k])
            S.op('act', lambda e: e.activation(out=m2[:, :], in_=m2[:, :], func=AF.Exp, scale=-0.5), reads=[m2k], writes=[m2k])
            S.op('dve', lambda e: e.tensor_tensor(out=yv, in0=yv, in1=mu[:, :].unsqueeze(1).broadcast_to([128, 4, TT]), op=ALU.subtract),
                 reads=yk + [muk], writes=yk)
            S.op('dve', lambda e: e.tensor_tensor(out=yv, in0=yv, in1=m2[:, :].unsqueeze(1).broadcast_to([128, 4, TT]), op=ALU.mult),
                 reads=yk + [m2k], writes=yk)
            for cc in range(4):
                S.op('dve', lambda e, cc=cc: e.tensor_scalar(
                    out=yconv[:, cc, ts(t)], in0=yconv[:, cc, ts(t)], scalar1=gcol(cb + O_LNG, cc), scalar2=gcol(cb + O_LNB, cc),
                    op0=ALU.mult, op1=ALU.add), reads=[('y', cc, t), 'con'], writes=[('y', cc, t)])

        def LB_pre_b(t):
            yk = [('y', cc, t) for cc in range(4)]
            yv = yconv[:, :, ts(t)]
            S.op('act', lambda e: e.activation(out=yv, in_=yv, func=AF.Silu), reads=yk, writes=yk)
            S.op('act', lambda e: e.activation(out=sqb[:, 0:4, :], in_=yv, func=AF.Square), reads=yk, writes=['sqb_lo'])

        def LB_pe(t):
            rb, rbk = bank()
            mm_group(rb[:, :], rbk, [(ones[:, :], sqb[:, c, :], ['sqb_lo', 'ones']) for c in range(4)])
            lst[('B', t)] = (rb, rbk)

        def LC(t):
            rb, rbk = lst[('B', t)]
            r2, r2k = grs()
            S.op('act', lambda e: e.activation(out=r2[:, :], in_=rb[:, :], func=AF.Ln, bias=EPS, scale=1.0 / 512.0), reads=[rbk], writes=[r2k])
            S.op('act', lambda e: e.activation(out=r2[:, :], in_=r2[:, :], func=AF.Exp, scale=-0.5), reads=[r2k], writes=[r2k])
            for cc in range(4):
                S.op('dve', lambda e, cc=cc: e.scalar_tensor_tensor(
                    out=mixedB[:, cc, ts(t)], in0=yconv[:, cc, ts(t)], scalar=gcol(cb + O_CONVN, cc), in1=r2[:, :], op0=ALU.mult, op1=ALU.mult),
                    reads=[('y', cc, t), r2k, 'con'], writes=[('mB', cc, t)])

        wo_slots = {}
        wo_done = [0]
        wo_order = [(mp, mm, t) for t in range(NTL) for mp in range(4) for mm in range(2)]

        def wo_group(mp, mm, t):
            if mp not in wo_slots:
                wo_slots[mp] = loadA(wslice(wout, l, mp * 256, mp * 256 + 256))
            wa, wak = wo_slots[mp]
            m = mp * 2 + mm
            cs = slice(mm * 128, (mm + 1) * 128)
            ob, obk = bank()
            terms = [(wa[:, c, cs], mA[:, c, ts(t)], [wak] + [('mA', qb) for qb in range(t * 4, t * 4 + 4)]) for c in range(4)]
            terms += [(wa[:, 4 + cc, cs], mixedB[:, cc, ts(t)], [wak, ('mB', cc, t)]) for cc in range(4)]
            mm_group(ob[:, :], obk, terms)
            S.op('dve', lambda e: e.tensor_tensor(out=hT[:, m, ts(t)], in0=ob[:, :], in1=hT[:, m, ts(t)], op=ALU.add),
                 reads=[obk, ('h', m, t)], writes=[('h', m, t)])
            wo_done[0] += 1

        pre_sched = {4: [lambda: LA_pre(0)], 5: [lambda: LB_pre(0)], 7: [lambda: LC(0)], 8: [lambda: LA_pre(1)], 9: [lambda: LB_pre(1)]}
        pe_sched = {4: [lambda: LA_pe(0)], 6: [lambda: LB_pe(0)], 8: [lambda: LA_pe(1)]}
        late_sched = {5: [lambda: LB_pre_b(0)], 9: [lambda: LB_pre_b(1)]}
        build_dg(0)
        for i in range(NBLK + 2):
            if i < NBLK:
                s_stage(i)
            if i + 1 < NBLK:
                build_dg(i + 1)
            for f in pre_sched.get(i, []):
                f()
            if i < NBLK:
                conv_part(i, 0)
            if 1 <= i < NBLK + 1:
                pv_stage(i - 1)
            if i < NBLK:
                conv_part(i, 1)
            for f in late_sched.get(i, []):
                f()
            if 2 <= i < NBLK + 2:
                norm_stage(i - 2)
            if i >= NBLK:
                for mp_, mm_, t_ in wo_order[wo_done[0]:wo_done[0] + 2]:
                    wo_group(mp_, mm_, t_)
            for f in pe_sched.get(i, []):
                f()
        for mp_, mm_, t_ in wo_order[wo_done[0]:NBLK]:
            wo_group(mp_, mm_, t_)
        LB_pe(1)
        LC(1)

        for mp, mm, t in wo_order[wo_done[0]:]:
            wo_group(mp, mm, t)
            if (mp, mm, t) == (0, 0, 1) and tail_hook is not None:
                tail_hook()

    def ple_prefetch(l, st):
        t0 = st * T
        S.phase('MB', ['ppT', 'wpl'])
        S.op('pool', lambda e: e.dma_start(out=ppT[:, :, :], in_=pT[l].rearrange("(kc p) t -> p kc t", p=128)[:, :, t0:t0 + T]),
             writes=['ppT'], dma_sem=sem_pp)
        S.op('pool', lambda e: e.dma_start(out=wpl[:, :, :], in_=wple[l].rearrange("(kc p) n -> p kc n", p=128)),
             writes=['wpl'], dma_sem=sem_wpl)

    def ple(l, st, tail_hook=None):
        cb = l * LW
        t0 = st * T
        S.phase('xnbuf', [('xn', c, t) for c in range(8) for t in range(NTL)])
        S.phase('R1', [('e', m, t) for m in range(8) for t in range(NTL)] + [('sqE', t) for t in range(NTL)])
        sqE = [R1[:, 8192 + t * 2048:8192 + (t + 1) * 2048].bitcast(BF16).rearrange("p (c n) -> p c n", c=8) for t in range(NTL)]
        for t in range(NTL):
            for m in range(8):
                eb, ebk = bank()
                mm_group(eb[:, :], ebk, [(wpl[:, kc, m * 128:(m + 1) * 128], ppT[:, kc, ts(t)], ['wpl', 'ppT']) for kc in range(2)])
                S.op('dve', lambda e, eb=eb, m=m, t=t: e.tensor_copy(out=eT[:, m, ts(t)], in_=eb[:, :]), reads=[ebk], writes=[('e', m, t)])
        for t in range(NTL):
            if t == 0 and early['hb0']:
                early['hb0'] = False
                continue
            S.op('act', lambda e, t=t: e.activation(out=xn[:, :, ts(t)], in_=hT[:, :, ts(t)], func=AF.Copy),
                 reads=hkeys(t), writes=xnkeys(t))
        for t in range(NTL):
            S.op('act', lambda e, t=t: e.activation(out=sqE[t][:, :, :], in_=eT[:, :, ts(t)], func=AF.Square),
                 reads=[('e', m, t) for m in range(8)], writes=[('sqE', t)])
        groups = [(mp, mm, t) for t in range(NTL) for mp in range(4) for mm in range(2)]
        gstate = {}
        wslots = {}

        def gate_pe(i):
            mp, mm, t = groups[i]
            if mp not in wslots:
                wslots[mp] = loadA(wslice(wgate, l, mp * 256, mp * 256 + 256))
            wa, wk = wslots[mp]
            gb, gbk = bank()
            mm_group(gb[:, :], gbk, [(wa[:, kc, mm * 128:(mm + 1) * 128], xn[:, kc, ts(t)], [wk, ('xn', kc, t)]) for kc in range(8)])
            gstate[i] = (gb, gbk)

        def gate_post(i):
            mp, mm, t = groups[i]
            m = mp * 2 + mm
            gb, gbk = gstate[i]
            sg, sgk = gtmp()
            S.op('act', lambda e, sg=sg, gb=gb: e.activation(out=sg[:, :], in_=gb[:, :], func=AF.Sigmoid), reads=[gbk], writes=[sgk])
            S.op('dve', lambda e, sg=sg, m=m, t=t: e.tensor_tensor(out=sg[:, :], in0=sg[:, :], in1=eT[:, m, ts(t)], op=ALU.mult),
                 reads=[sgk, ('e', m, t)], writes=[sgk])
            S.op('dve', lambda e, sg=sg, m=m, t=t: e.tensor_tensor(out=hT[:, m, ts(t)], in0=hT[:, m, ts(t)], in1=sg[:, :], op=ALU.add),
                 reads=[sgk, ('h', m, t)], writes=[('h', m, t)])

        LEAD = 4
        for i in range(LEAD):
            gate_pe(i)
        for t in range(NTL):
            b_, bk_ = bank()
            mm_group(b_[:, :], bk_, [(ones[:, :], sqE[t][:, c, :], [('sqE', t), 'ones']) for c in range(8)])
            r, rk = grs()
            S.op('act', lambda e, r=r, b_=b_: e.activation(out=r[:, :], in_=b_[:, :], func=AF.Ln, bias=EPS, scale=1.0 / float(D)), reads=[bk_], writes=[rk])
            S.op('act', lambda e, r=r: e.activation(out=r[:, :], in_=r[:, :], func=AF.Exp, scale=-0.5), reads=[rk], writes=[rk])
            for m in range(8):
                S.op('dve', lambda e, m=m, t=t, r=r: e.scalar_tensor_tensor(
                    out=eT[:, m, ts(t)], in0=eT[:, m, ts(t)], scalar=gcol(cb + O_PLE, m), in1=r[:, :], op0=ALU.mult, op1=ALU.mult),
                    reads=[('e', m, t), rk, 'con'], writes=[('e', m, t)])
        for i in range(len(groups)):
            if i + LEAD < len(groups):
                gate_pe(i + LEAD)
            gate_post(i)
            if i == 8 and tail_hook is not None:
                tail_hook()

    OKEYS = [('o', c, t) for c in range(8) for t in range(NTL)]

    def final_tile(t):
        r, rk = rms_rstd(hT[:, :, ts(t)], hkeys(t), 8, 128, float(D), TT)
        for c in range(8):
            S.op('dve', lambda e, c=c, r=r: e.scalar_tensor_tensor(
                out=eT[:, c, ts(t)], in0=hT[:, c, ts(t)], scalar=gcol(O_FINAL, c), in1=r[:, :], op0=ALU.mult, op1=ALU.mult),
                reads=[('h', c, t), rk, 'con'], writes=[('o', c, t)])

    def early_final(st):
        S.phase('R1', OKEYS)
        final_tile(0)
        early['final0'] = True
        if st + 1 < nst:
            load_x_tile(st + 1, 0)
            early['x0'] = True

    def final(st):
        t0 = st * T
        S.phase('R1', OKEYS)
        for t in range(NTL):
            if t == 0 and early.get('final0'):
                early['final0'] = False
                continue
            final_tile(t)

        def store(extra_reads=()):
            S.op('sp', lambda e: e.dma_start(out=outT.rearrange("(c p) t -> p c t", p=128)[:, :, t0:t0 + T], in_=eT[:, :, :]),
                 reads=OKEYS + list(extra_reads), dma_sem=sem_out)
        return store

    sem_xt = [sem_x, S.new_dma_sem()]

    def load_x_tile(st, t):
        c0 = st * T + t * TT
        S.op('sp', lambda e: e.dma_start(out=hT[:, :, ts(t)], in_=xT.rearrange("(c p) t -> p c t", p=128)[:, :, c0:c0 + TT]),
             writes=[('h', c, t) for c in range(8)], dma_sem=sem_xt[t])

    def load_x(st):
        for t in range(NTL):
            if t == 0 and early.get('x0'):
                early['x0'] = False
                continue
            load_x_tile(st, t)

    load_x(0)
    for st in range(nst):
        for l in range(nlayer):
            ffn(l, 0, after_norm=(lambda st=st: rope_tables(st)) if l == 0 else None,
                tail_hook=lambda l=l: early_norm(l * LW + O_MIX))
            mixer(l, st, tail_hook=lambda l=l: early_norm(l * LW + O_FFN2))
            ffn(l, 1, after_norm=lambda l=l, st=st: ple_prefetch(l, st), tail_hook=early_hb)
            ple(l, st, tail_hook=(lambda l=l: early_norm((l + 1) * LW + O_FFN1)) if l + 1 < nlayer else (lambda st=st: early_final(st)))
        store = final(st)
        if st + 1 < nst:
            load_x(st + 1)
            store(extra_reads=[('h', 0, NTL - 1)])
        else:
            store()
    S.final_wait('sp', [(sem_out, S.dcnt[sem_out])])
    S.emit()
    return nc


def _chunks(v, n, p=128):
    return np.ascontiguousarray(np.asarray(v, np.float32).reshape(n, p).T)


def _host_consts(inp):
    con = np.zeros((128, NCON), np.float32)
    for l in range(NLAYER):
        b = l * LW
        con[:, b + O_FFN1:b + O_FFN1 + 8] = _chunks(inp["ffn1_norm"][l], 8)
        con[:, b + O_MIX:b + O_MIX + 8] = _chunks(inp["mix_norm"][l], 8)
        con[:, b + O_FFN2:b + O_FFN2 + 8] = _chunks(inp["ffn2_norm"][l], 8)
        con[:, b + O_PLE:b + O_PLE + 8] = _chunks(inp["ple_norm"][l], 8)
        a = _chunks(inp["attn_out_norm"][l], 8, 64)
        con[:, b + O_ATTN:b + O_ATTN + 4] = np.concatenate([a[:, 0:4], a[:, 4:8]], axis=0)
        con[:, b + O_CONVN:b + O_CONVN + 4] = _chunks(inp["conv_out_norm"][l], 4)
        con[:, b + O_LNG:b + O_LNG + 4] = _chunks(inp["conv_ln_g"][l], 4)
        con[:, b + O_LNB:b + O_LNB + 4] = _chunks(inp["conv_ln_b"][l], 4)
        con[:, b + O_CB:b + O_CB + 4] = _chunks(inp["conv_b"][l], 4)
        cw = np.asarray(inp["conv_w"][l], np.float32).reshape(CW, 4, 128)
        con[:, b + O_CW:b + O_CW + CW * 4] = cw.transpose(2, 1, 0).reshape(128, CW * 4)
        con[:, b + O_SINK:b + O_SINK + 8] = np.broadcast_to(np.asarray(inp["sinks"][l], np.float32)[None, :], (128, 8))
    con[:, O_FINAL:O_FINAL + 8] = _chunks(inp["final_norm"], 8)
    inv = (500000.0 ** (-np.arange(0, 16, 2, dtype=np.float32) / 16.0)).astype(np.float32)
    for p in range(128):
        d = p % 64
        con[p, O_PHC] = np.float32(np.pi / 2)
        if d < 16:
            con[p, O_FREQ] = inv[d % 8]
            con[p, O_PHS] = np.float32(np.pi) if d < 8 else 0.0
    return con


def _host_mask():
    j = np.arange(128)[:, None]
    i = np.arange(128)[None, :]
    m = np.zeros((128, 2, 128), np.float32)
    m[:, 0, :] = (j > i)
    m[:, 1, :] = (j <= i)
    return m


def _host_win(w_in):
    w = np.asarray(w_in, np.float32)
    L = w.shape[0]

    def headcols(base, h):
        return base + h * 64 + np.arange(64)

    def swapped(cols):
        c = cols.copy()
        c[0:8] = cols[8:16]
        c[8:16] = cols[0:8]
        return c
    idx = []
    for c in range(4):
        q = np.concatenate([headcols(0, c), headcols(0, 4 + c)])
        qs = np.concatenate([swapped(headcols(0, c)), swapped(headcols(0, 4 + c))])
        idx += [q, qs]
    k = np.concatenate([headcols(512, 0), headcols(512, 1)])
    ks = np.concatenate([swapped(headcols(512, 0)), swapped(headcols(512, 1))])
    idx += [k, ks]
    for cc in range(4):
        idx += [768 + cc * 128 + np.arange(128), 1280 + cc * 128 + np.arange(128)]
    idx += [640 + np.arange(128), 640 + np.arange(128)]
    idx = np.concatenate(idx)
    return np.ascontiguousarray(w[:, :, idx])


def _host_wout(w_out):
    w = np.asarray(w_out, np.float32)
    idx = []
    for c in range(4):
        idx += [c * 64 + np.arange(64), (4 + c) * 64 + np.arange(64)]
    idx.append(512 + np.arange(512))
    idx = np.concatenate(idx)
    return np.ascontiguousarray(w[:, idx, :])


def _prep_shared(inp):
    return {
        "wgu1": np.ascontiguousarray(inp["ffn1_w_gate_up"], np.float32),
        "wgu2": np.ascontiguousarray(inp["ffn2_w_gate_up"], np.float32),
        "wd1": np.ascontiguousarray(inp["ffn1_w_down"], np.float32),
        "wd2": np.ascontiguousarray(inp["ffn2_w_down"], np.float32),
        "win": _host_win(inp["w_in"]),
        "wout": _host_wout(inp["w_out"]),
        "wple": np.ascontiguousarray(inp["w_ple"], np.float32),
        "wgate": np.ascontiguousarray(inp["w_ple_gate"], np.float32),
        "con": _host_consts(inp),
        "mask": _host_mask(),
        "ident": np.eye(128, dtype=np.float32),
    }


def run(inp, cores, seq=SEQ):
    nst = seq // T
    nc = build_program(nst=nst, seq=seq)
    shared = _prep_shared(inp)
    in_maps = []
    for b in cores:
        m = dict(shared)
        m["xT"] = np.ascontiguousarray(np.asarray(inp["x"][b, :seq], np.float32).T)
        m["pT"] = np.ascontiguousarray(np.asarray(inp["p"][:, b, :seq], np.float32).transpose(0, 2, 1))
        m["pos"] = np.ascontiguousarray(np.asarray(inp["positions"][b, :seq], np.int32)[None, :])
        in_maps.append(m)
    res = run_bass_kernel_spmd(nc, in_maps, core_ids=list(range(len(cores))))
    outs = [np.ascontiguousarray(r["outT"].T) for r in res.results]
    return np.stack(outs, axis=0)


def kernel(**inputs):
    out = run(inputs, list(range(8)), SEQ)
    return out.astype(np.float32)
```
